# Optimizing a Trainium2 kernel written in Bass

```python
import math
import jax
import jax.numpy as jnp
from jax import lax
import numpy as np

D_MODEL = 1024
BATCH = 16
SEQ = 2048
DEPTH = 2

HEAD_DIM = 64
SB_HEADS = 4
MLA_HEADS = 6
MLA_Q_RANK = 256
MLA_KV_RANK = 128
MLA_NOPE = 64
MLA_ROPE = 32
MLA_V = 64
MLA_QK = MLA_NOPE + MLA_ROPE
ROPE_THETA = 10000.0
SW_HEADS = 6
SW_KV_HEADS = 2
WINDOW = 128
REL_BUCKETS = 32
REL_MAX_DIST = 128
BLOCK = 128
D_FF = 2816
CONV_W = 3
EPS = 1e-6
NEG = -1e30

D_MIX = SB_HEADS * HEAD_DIM + MLA_HEADS * MLA_V + SW_HEADS * HEAD_DIM
IN_SPLITS = (SB_HEADS * HEAD_DIM, SB_HEADS * HEAD_DIM, SB_HEADS * HEAD_DIM,
             MLA_Q_RANK, MLA_KV_RANK, MLA_ROPE,
             SW_HEADS * HEAD_DIM, SW_KV_HEADS * HEAD_DIM, SW_KV_HEADS * HEAD_DIM)
D_IN = 3 * SB_HEADS * HEAD_DIM + MLA_Q_RANK + MLA_KV_RANK + MLA_ROPE + (SW_HEADS + 2 * SW_KV_HEADS) * HEAD_DIM

kernel_name = 'hybrid_sb_mla_swa_convffn_block'


def rms_norm(x, g):
    xf = x.astype(jnp.float32)
    y = xf * lax.rsqrt(jnp.mean(xf * xf, axis=-1, keepdims=True) + EPS)
    return (y * g.astype(jnp.float32)).astype(x.dtype)


def split_cols(t, sizes):
    out = []
    start = 0
    for n in sizes:
        out.append(t[..., start:start + n])
        start += n
    return out


def apply_rope(x, positions):
    half = x.shape[-1] // 2
    inv_freq = jnp.power(ROPE_THETA, -jnp.arange(half, dtype=jnp.float32) / half)
    ang = positions.astype(jnp.float32)[..., None] * inv_freq
    cos = jnp.cos(ang)[:, :, None, :]
    sin = jnp.sin(ang)[:, :, None, :]
    x1 = x[..., :half].astype(jnp.float32)
    x2 = x[..., half:].astype(jnp.float32)
    out = jnp.concatenate([x1 * cos - x2 * sin, x1 * sin + x2 * cos], axis=-1)
    return out.astype(x.dtype)


def t5_causal_bucket(dist):
    max_exact = REL_BUCKETS // 2
    n = jnp.maximum(dist, 0)
    nf = jnp.maximum(n, 1).astype(jnp.float32)
    large = max_exact + (jnp.log(nf / max_exact) / math.log(REL_MAX_DIST / max_exact)
                         * (REL_BUCKETS - max_exact)).astype(jnp.int32)
    large = jnp.minimum(large, REL_BUCKETS - 1)
    return jnp.where(n < max_exact, n, large)


def window_rel_bias(rel_table):
    a = jnp.arange(BLOCK)[:, None]
    b = jnp.arange(2 * BLOCK)[None, :]
    bucket = t5_causal_bucket(BLOCK + a - b)
    return jnp.transpose(rel_table[bucket], (2, 0, 1))


def stick_breaking_attention(q, k, v):
    B, S, H, D = q.shape
    scale = D ** -0.5
    outs = []
    for i in range(S // BLOCK):
        t0 = i * BLOCK
        end = t0 + BLOCK
        z = jnp.einsum('bqhd,bkhd->bhqk', q[:, t0:end], k[:, :end]).astype(jnp.float32) * scale
        strict = jnp.arange(end)[None, :] < (t0 + jnp.arange(BLOCK))[:, None]
        log_keep = jnp.where(strict, -jax.nn.softplus(z), 0.0)
        suffix = lax.cumsum(log_keep, axis=log_keep.ndim - 1, reverse=True) - log_keep
        weights = jnp.where(strict, jnp.exp(jax.nn.log_sigmoid(z) + suffix), 0.0)
        outs.append(jnp.einsum('bhqk,bkhd->bqhd', weights.astype(v.dtype), v[:, :end]))
    return jnp.concatenate(outs, axis=1)


def causal_softmax_attention(q, k, v):
    B, S, H, Dk = q.shape
    scale = Dk ** -0.5
    outs = []
    for i in range(S // BLOCK):
        t0 = i * BLOCK
        end = t0 + BLOCK
        s = jnp.einsum('bqhd,bkhd->bhqk', q[:, t0:end], k[:, :end]).astype(jnp.float32) * scale
        causal = jnp.arange(end)[None, :] <= (t0 + jnp.arange(BLOCK))[:, None]
        p = jax.nn.softmax(jnp.where(causal, s, NEG), axis=-1)
        outs.append(jnp.einsum('bhqk,bkhd->bqhd', p.astype(v.dtype), v[:, :end]))
    return jnp.concatenate(outs, axis=1)


def sliding_window_sink_attention(q, k, v, sinks, rel_bias):
    B, S, H, D = q.shape
    G = k.shape[2]
    R = H // G
    nb = S // BLOCK
    qb = q.reshape(B, nb, BLOCK, G, R, D)

    def band(t):
        tp = jnp.concatenate([jnp.zeros((B, BLOCK, G, D), t.dtype), t], axis=1)
        tp = tp.reshape(B, nb + 1, BLOCK, G, D)
        return jnp.concatenate([tp[:, :-1], tp[:, 1:]], axis=2)

    kb = band(k)
    vb = band(v)
    s = jnp.einsum('bnqgrd,bnkgd->bngrqk', qb, kb).astype(jnp.float32) * (D ** -0.5)
    s = s + rel_bias.astype(jnp.float32).reshape(G, R, BLOCK, 2 * BLOCK)[None, None]
    dist = BLOCK + jnp.arange(BLOCK)[:, None] - jnp.arange(2 * BLOCK)[None, :]
    in_window = (dist >= 0) & (dist < WINDOW)
    key_pos = (jnp.arange(nb)[:, None] - 1) * BLOCK + jnp.arange(2 * BLOCK)[None, :]
    valid = in_window[None] & (key_pos >= 0)[:, None, :]
    s = jnp.where(valid[None, :, None, None], s, NEG)
    sink = jnp.broadcast_to(sinks.astype(jnp.float32).reshape(1, 1, G, R, 1, 1), s.shape[:-1] + (1,))
    p = jax.nn.softmax(jnp.concatenate([s, sink], axis=-1), axis=-1)[..., :-1]
    o = jnp.einsum('bngrqk,bnkgd->bnqgrd', p.astype(v.dtype), vb)
    return o.reshape(B, S, H, D)


def causal_depthwise_conv(u, w, b):
    C = u.shape[-1]
    y = lax.conv_general_dilated(u, w[:, None, :].astype(u.dtype), window_strides=(1,),
                                 padding=[(CONV_W - 1, 0)],
                                 dimension_numbers=('NWC', 'WIO', 'NWC'),
                                 feature_group_count=C)
    return y + b


def hybrid_layer(x, cond, positions, rel_bias, norm1_g, norm2_g, w_ada, b_ada, w_in,
                 mla_cq_g, w_uq, mla_ckv_g, w_ukv, mla_qn_g, mla_kn_g, sw_qn_g, sw_kn_g,
                 sw_sinks, w_out, w_up, conv_w, conv_b, w_down):
    B, S, _ = x.shape
    mods = jnp.einsum('bd,de->be', jax.nn.silu(cond), w_ada) + b_ada
    shift1, scale1, gate1, shift2, scale2, gate2 = jnp.split(mods[:, None, :], 6, axis=-1)

    h = rms_norm(x, norm1_g) * (1.0 + scale1) + shift1
    proj = jnp.einsum('bsd,de->bse', h, w_in)
    sb_q, sb_k, sb_v, cq, ckv, k_rope, sw_q, sw_k, sw_v = split_cols(proj, IN_SPLITS)

    sb_shape = (B, S, SB_HEADS, HEAD_DIM)
    o_a = stick_breaking_attention(sb_q.reshape(sb_shape), sb_k.reshape(sb_shape), sb_v.reshape(sb_shape))

    q_b = jnp.einsum('bsr,re->bse', rms_norm(cq, mla_cq_g), w_uq).reshape(B, S, MLA_HEADS, MLA_QK)
    kv_b = jnp.einsum('bsr,re->bse', rms_norm(ckv, mla_ckv_g), w_ukv).reshape(B, S, MLA_HEADS, MLA_NOPE + MLA_V)
    k_nope = kv_b[..., :MLA_NOPE]
    v_b = kv_b[..., MLA_NOPE:]
    k_rope_h = jnp.broadcast_to(k_rope[:, :, None, :], (B, S, MLA_HEADS, MLA_ROPE))
    k_b = jnp.concatenate([k_nope, k_rope_h], axis=-1)
    q_b = rms_norm(q_b, mla_qn_g)
    k_b = rms_norm(k_b, mla_kn_g)
    q_b = jnp.concatenate([q_b[..., :MLA_NOPE], apply_rope(q_b[..., MLA_NOPE:], positions)], axis=-1)
    k_b = jnp.concatenate([k_b[..., :MLA_NOPE], apply_rope(k_b[..., MLA_NOPE:], positions)], axis=-1)
    o_b = causal_softmax_attention(q_b, k_b, v_b)

    q_c = rms_norm(sw_q.reshape(B, S, SW_HEADS, HEAD_DIM), sw_qn_g)
    k_c = rms_norm(sw_k.reshape(B, S, SW_KV_HEADS, HEAD_DIM), sw_kn_g)
    v_c = sw_v.reshape(B, S, SW_KV_HEADS, HEAD_DIM)
    o_c = sliding_window_sink_attention(q_c, k_c, v_c, sw_sinks, rel_bias)

    mix = jnp.concatenate([o_a.reshape(B, S, -1), o_b.reshape(B, S, -1), o_c.reshape(B, S, -1)], axis=-1)
    x = x + gate1 * jnp.einsum('bse,ed->bsd', mix, w_out)

    h2 = rms_norm(x, norm2_g) * (1.0 + scale2) + shift2
    u = causal_depthwise_conv(jnp.einsum('bsd,df->bsf', h2, w_up), conv_w, conv_b)
    g = u[..., :D_FF]
    val = u[..., D_FF:]
    y = jnp.einsum('bsf,fd->bsd', jax.nn.silu(g) * val, w_down)
    return x + gate2 * y


def setup_inputs(seed: int = 0) -> dict:
    key = jax.random.key(seed)
    ks = jax.random.split(key, 24)
    f32 = jnp.float32
    L = DEPTH
    D = D_MODEL

    def nrm(k, shape, scale):
        return jax.random.normal(k, shape, f32) * scale

    def gain(k, shape):
        return 1.0 + 0.02 * jax.random.normal(k, shape, f32)

    x = nrm(ks[0], (BATCH, SEQ, D), 1.0)
    c = nrm(ks[1], (BATCH, D), 1.0)
    offsets = jax.random.randint(ks[2], (BATCH, 1), 0, SEQ, dtype=jnp.int32)
    positions = offsets + jnp.arange(SEQ, dtype=jnp.int32)[None, :]
    rel_table = nrm(ks[3], (REL_BUCKETS, SW_HEADS), 0.5)
    norm1_g = gain(ks[4], (L, D))
    norm2_g = gain(ks[5], (L, D))
    w_ada = nrm(ks[6], (L, D, 6 * D), 0.5 * D ** -0.5)
    b_ada = nrm(ks[7], (L, 6 * D), 0.02)
    w_in = nrm(ks[8], (L, D, D_IN), D ** -0.5)
    mla_cq_g = gain(ks[9], (L, MLA_Q_RANK))
    w_uq = nrm(ks[10], (L, MLA_Q_RANK, MLA_HEADS * MLA_QK), MLA_Q_RANK ** -0.5)
    mla_ckv_g = gain(ks[11], (L, MLA_KV_RANK))
    w_ukv = nrm(ks[12], (L, MLA_KV_RANK, MLA_HEADS * (MLA_NOPE + MLA_V)), MLA_KV_RANK ** -0.5)
    mla_qn_g = gain(ks[13], (L, MLA_QK))
    mla_kn_g = gain(ks[14], (L, MLA_QK))
    sw_qn_g = gain(ks[15], (L, HEAD_DIM))
    sw_kn_g = gain(ks[16], (L, HEAD_DIM))
    sw_sinks = nrm(ks[17], (L, SW_HEADS), 1.0)
    w_out = nrm(ks[18], (L, D_MIX, D), D_MIX ** -0.5)
    w_up = nrm(ks[19], (L, D, 2 * D_FF), D ** -0.5)
    conv_w = nrm(ks[20], (L, CONV_W, 2 * D_FF), CONV_W ** -0.5)
    conv_b = nrm(ks[21], (L, 2 * D_FF), 0.02)
    w_down = nrm(ks[22], (L, D_FF, D), D_FF ** -0.5)
    return {'x': x, 'c': c, 'positions': positions, 'rel_table': rel_table,
            'norm1_g': norm1_g, 'norm2_g': norm2_g, 'w_ada': w_ada, 'b_ada': b_ada,
            'w_in': w_in, 'mla_cq_g': mla_cq_g, 'w_uq': w_uq, 'mla_ckv_g': mla_ckv_g,
            'w_ukv': w_ukv, 'mla_qn_g': mla_qn_g, 'mla_kn_g': mla_kn_g,
            'sw_qn_g': sw_qn_g, 'sw_kn_g': sw_kn_g, 'sw_sinks': sw_sinks,
            'w_out': w_out, 'w_up': w_up, 'conv_w': conv_w, 'conv_b': conv_b,
            'w_down': w_down}


def reference(x, c, positions, rel_table, norm1_g, norm2_g, w_ada, b_ada, w_in,
              mla_cq_g, w_uq, mla_ckv_g, w_ukv, mla_qn_g, mla_kn_g, sw_qn_g, sw_kn_g,
              sw_sinks, w_out, w_up, conv_w, conv_b, w_down):
    rel_bias = window_rel_bias(rel_table)
    for l in range(DEPTH):
        x = hybrid_layer(x, c, positions, rel_bias, norm1_g[l], norm2_g[l], w_ada[l], b_ada[l],
                         w_in[l], mla_cq_g[l], w_uq[l], mla_ckv_g[l], w_ukv[l], mla_qn_g[l],
                         mla_kn_g[l], sw_qn_g[l], sw_kn_g[l], sw_sinks[l], w_out[l], w_up[l],
                         conv_w[l], conv_b[l], w_down[l])
    return x
```

```python
import contextlib
import math
import numpy as np
import ml_dtypes
import concourse.bass as bass
import concourse.mybir as mybir
from concourse.bass_utils import run_bass_kernel_spmd

F32 = mybir.dt.float32
BF16 = mybir.dt.bfloat16
I32 = mybir.dt.int32
AF = mybir.ActivationFunctionType
ALU = mybir.AluOpType
AX = mybir.AxisListType

N_DMA_SEMS = 32
N_HW_SEMS = 24
D = 1024
D_IN = 1824
D_FF = 2816
NFC = 22
EPS = 1e-6
N_CORES = 8


class Res:
    __slots__ = ("name", "lw", "rd", "excl")

    def __init__(self, name, excl=False):
        self.name = name
        self.lw = None
        self.rd = []
        self.excl = excl


class Op:
    __slots__ = ("eng", "idx", "fn", "deps", "odeps", "signaled", "dma", "dsem", "dval", "prewait",
                 "cost", "lat", "pidx", "npred", "succ", "ready", "fin", "epoch", "gate_of", "free", "tail")

    def __init__(self, eng, fn, dma=False):
        self.eng = eng
        self.fn = fn
        self.deps = []
        self.odeps = []
        self.signaled = False
        self.dma = dma
        self.dsem = None
        self.dval = None
        self.prewait = None
        self.cost = 100.0
        self.lat = 0.0
        self.epoch = 0
        self.gate_of = None
        self.free = False


class _Rec:
    def __init__(self):
        self.calls = []

    def __getattr__(self, name):
        def f(*a, **k):
            self.calls.append((name, a, k))
            return self
        return f


def _free(ap):
    n = 1
    for d in ap.shape[1:]:
        n *= int(d)
    return n


def _is_psum(ap):
    return "psum" in str(ap.space).lower() or "PSUM" in str(ap.space)


def _estimate(o):
    rec = _Rec()
    try:
        o.fn(rec)
    except Exception:
        return
    if not rec.calls:
        return
    name, a, k = rec.calls[0]
    aps = [x for x in list(a) + list(k.values()) if hasattr(x, "shape") and hasattr(x, "dtype")]
    if not aps:
        return
    out = k.get("out", a[0] if a else aps[0])
    if o.dma:
        nbytes = _free(out) * int(out.shape[0]) * mybir.dt.size(out.dtype)
        o.cost = 60.0
        o.lat = 2000.0 + nbytes / 160.0
        return
    if o.eng == "pe":
        rhs = k.get("rhs", a[2] if len(a) > 2 else out)
        o.cost = max(_free(rhs), 64) / 2.0 + 16.0
    elif o.eng == "act":
        o.cost = (_free(out) + 224.0) / 1.2
    else:
        n = _free(out)
        psum = any(_is_psum(x) for x in aps)
        small = all(mybir.dt.size(x.dtype) == 2 for x in aps)
        speed = 1.0
        if not psum:
            if name in ("tensor_copy", "tensor_scalar", "memset"):
                speed = 4.0 if small else 2.0
            elif small:
                speed = 2.0
        base = 120.0 if psum else 70.0
        o.cost = (base + n / speed) / 0.96
        if o.eng == "pool":
            o.cost *= 2.0


class Sched:
    ENGS = ("pe", "act", "dve", "pool", "sp")

    def __init__(self, nc):
        self.nc = nc
        self.prog = {e: [] for e in self.ENGS}
        self.dma_rr = 0
        self.dma_rr_sw = N_HW_SEMS
        self.dma_cnt = [0] * N_DMA_SEMS
        self.dma_last = [None] * N_DMA_SEMS
        self.need_gate = {e: None for e in self.ENGS}
        self.epoch = 0
        self.epoch_ops = [[]]
        self.nops = 0
        self.gate = {e: None for e in self.ENGS}
        self.warm_fn = None
        self.warm_dep = None
        self.warm_epochs = set()
        self.n_dummies = 0

    def _add(self, o):
        o.epoch = self.epoch
        if o.free:
            o.idx = len(self.prog[o.eng])
            o.pidx = self.nops
            self.nops += 1
            self.prog[o.eng].append(o)
            _estimate(o)
            return o
        if self.need_gate[o.eng] is not None:
            o.gate_of = self.need_gate[o.eng]
            self.need_gate[o.eng] = None
            self.gate[o.eng] = o
        elif self.gate[o.eng] is not None:
            o.odeps.append(self.gate[o.eng])
        o.idx = len(self.prog[o.eng])
        o.pidx = self.nops
        self.nops += 1
        self.prog[o.eng].append(o)
        self.epoch_ops[-1].append(o)
        _estimate(o)
        return o

    def op(self, eng, fn, reads=(), writes=()):
        o = Op(eng, fn)
        self._deps(o, reads, writes)
        return self._add(o)

    def dma(self, eng, fn, reads=(), writes=(), free=False):
        o = Op(eng, fn, dma=True)
        o.free = free
        self._deps(o, reads, writes)
        if eng == "pool":
            i = self.dma_rr_sw
            self.dma_rr_sw = N_HW_SEMS + (self.dma_rr_sw + 1 - N_HW_SEMS) % (N_DMA_SEMS - N_HW_SEMS)
        else:
            i = self.dma_rr
            self.dma_rr = (self.dma_rr + 1) % N_HW_SEMS
        o.prewait = self.dma_last[i]
        self.dma_cnt[i] += 16
        o.dsem = i
        o.dval = self.dma_cnt[i]
        self.dma_last[i] = o
        return self._add(o)

    def _dep(self, o, d):
        if d is o:
            return
        if d.eng == "pe" and o.eng == "pe" and not d.dma:
            for x in o.odeps:
                if x is d:
                    return
            o.odeps.append(d)
            return
        for x in o.deps:
            if x is d:
                return
        o.deps.append(d)
        if not d.dma:
            d.signaled = True

    def _deps(self, o, reads, writes):
        for r in reads:
            if r.lw is not None:
                self._dep(o, r.lw)
            if r.excl:
                for d in r.rd:
                    if d.eng != o.eng:
                        self._dep(o, d)
        for w in writes:
            if w.lw is not None:
                self._dep(o, w.lw)
            for d in w.rd:
                self._dep(o, d)
        for r in reads:
            r.rd.append(o)
        for w in writes:
            w.lw = o
            w.rd = []

    def barrier(self):
        for e in self.ENGS:
            self.need_gate[e] = self.epoch
        self.epoch += 1
        self.epoch_ops.append([])

    def schedule(self):
        SEM_LAT = 120.0
        allops = []
        for e in self.ENGS:
            allops.extend(self.prog[e])
        for o in allops:
            o.succ = []
            o.npred = 0
            o.ready = 0.0
            o.fin = None
        for o in allops:
            preds = list(o.deps) + list(o.odeps)
            if o.prewait is not None:
                preds.append(o.prewait)
            if o.gate_of is not None:
                preds.extend(self.epoch_ops[o.gate_of])
            o.npred = len(preds)
            for d in preds:
                d.succ.append(o)
        byp = sorted(allops, key=lambda o: o.pidx)
        for o in byp:
            o.tail = 0.0
        for o in reversed(byp):
            t = 0.0
            for sc in o.succ:
                if sc.tail > t:
                    t = sc.tail
            o.tail = t + o.cost + (o.lat if o.dma else 0.0) + SEM_LAT
        ready = {e: [] for e in self.ENGS}
        for o in allops:
            if o.npred == 0:
                ready[o.eng].append(o)
        tfree = {e: 0.0 for e in self.ENGS}
        xfer_free = [0.0]
        newprog = {e: [] for e in self.ENGS}
        remaining = len(allops)
        while remaining:
            best = None
            best_start = None
            for e in self.ENGS:
                lst = ready[e]
                if not lst:
                    continue
                tf = tfree[e]
                cand = None
                for o in lst:
                    if o.ready <= tf:
                        if cand is None or cand.ready > tf or (PRIO_CP[0] and o.tail > cand.tail) or (not PRIO_CP[0] and o.pidx < cand.pidx):
                            cand = o
                    elif cand is None or (cand.ready > tf and (o.ready, o.pidx) < (cand.ready, cand.pidx)):
                        cand = o
                st = max(tf, cand.ready)
                if best is None or st < best_start or (st == best_start and cand.pidx < best.pidx):
                    best, best_start = cand, st
            o = best
            e = o.eng
            ready[e].remove(o)
            start = best_start
            tfree[e] = start + o.cost
            if o.dma:
                xs = max(start + o.cost, xfer_free[0])
                xfer_free[0] = xs + max(o.lat - 2000.0, 0.0)
                o.fin = xfer_free[0] + 2000.0
            else:
                o.fin = start + o.cost
            newprog[e].append(o)
            remaining -= 1
            for sc in o.succ:
                lat = 0.0 if (sc.eng == o.eng and not o.dma and o.eng == "pe") else SEM_LAT
                t = o.fin + lat
                if t > sc.ready:
                    sc.ready = t
                sc.npred -= 1
                if sc.npred == 0:
                    ready[sc.eng].append(sc)
        for e in self.ENGS:
            assert len(newprog[e]) == len(self.prog[e])
        if self.warm_fn is not None and self.warm_epochs:
            filled = []
            prev_fin = None
            prev_epoch = None
            ndum = 0
            for o in newprog["pe"]:
                st = o.fin - o.cost
                if prev_fin is not None and o.epoch in self.warm_epochs and not o.free and prev_epoch == o.epoch:
                    gap = st - prev_fin
                    n = int(WARM_FILL[0] * gap / WARM_COST)
                    for _ in range(min(n, 64)):
                        d = Op("pe", self.warm_fn)
                        d.free = True
                        d.epoch = o.epoch
                        d.cost = WARM_COST
                        d.fin = 0.0
                        d.pidx = -1
                        if self.warm_dep is not None:
                            d.deps.append(self.warm_dep)
                        filled.append(d)
                        ndum += 1
                filled.append(o)
                prev_fin = o.fin
                prev_epoch = o.epoch
            newprog["pe"] = filled
            self.n_dummies = ndum
        for e in self.ENGS:
            self.prog[e] = newprog[e]
            for i, o in enumerate(newprog[e]):
                o.idx = i
        self.est_ns = max(tfree.values())
        self._resolve_gates()

    def _resolve_gates(self):
        for e in self.ENGS:
            for g in self.prog[e]:
                if g.gate_of is None:
                    continue
                last = {}
                for o in self.epoch_ops[g.gate_of]:
                    if o.dma:
                        g.deps.append(o)
                    elif o.eng != g.eng:
                        if o.eng not in last or o.idx > last[o.eng].idx:
                            last[o.eng] = o
                for o in last.values():
                    g.deps.append(o)
                    o.signaled = True

    def emit(self, sems, dsems, final_wait_eng="sp"):
        nc = self.nc
        for e in self.ENGS:
            for o in self.prog[e]:
                o.signaled = False
        for e in self.ENGS:
            for o in self.prog[e]:
                last = {}
                for d in o.deps:
                    if d.dma:
                        continue
                    if d.eng not in last or d.idx > last[d.eng].idx:
                        last[d.eng] = d
                for d in last.values():
                    d.signaled = True
        cnt = {}
        for e in self.ENGS:
            c = 0
            arr = []
            for o in self.prog[e]:
                if o.signaled and not o.dma:
                    c += 1
                arr.append(c)
            cnt[e] = arr
        engobj = {"pe": "tensor", "act": "scalar", "dve": "vector", "pool": "gpsimd", "sp": "sync"}
        self.n_waits = 0
        with nc.Block() as block:
            for e in self.ENGS:
                ops = self.prog[e]
                if not ops and e != final_wait_eng:
                    continue

                def body(eng, e=e, ops=ops):
                    waited = {}

                    def wait_for(d):
                        if d.dma:
                            key = ("d", d.dsem)
                            val = d.dval
                            sem = dsems[d.dsem]
                        else:
                            key = ("e", d.eng)
                            val = cnt[d.eng][d.idx]
                            sem = sems[d.eng]
                        if waited.get(key, 0) >= val:
                            return
                        waited[key] = val
                        eng.wait_ge(sem, val)
                        self.n_waits += 1

                    for o in ops:
                        dl = list(o.deps)
                        if o.dma and o.prewait is not None:
                            dl.append(o.prewait)
                        best = {}
                        for d in dl:
                            if d.dma:
                                key, val = ("d", d.dsem), d.dval
                            else:
                                key, val = ("e", d.eng), cnt[d.eng][d.idx]
                            if key not in best or val > best[key][0]:
                                best[key] = (val, d)
                        for key in best:
                            wait_for(best[key][1])
                        ins = o.fn(eng)
                        if o.dma:
                            ins.then_inc(dsems[o.dsem], 16)
                        elif o.signaled:
                            ins.then_inc(sems[e], 1)
                    if e == final_wait_eng:
                        for i in range(N_DMA_SEMS):
                            if self.dma_last[i] is not None:
                                wait_for(self.dma_last[i])

                getattr(block, engobj[e])(body)


class Rot:
    def __init__(self, items):
        self.items = items
        self.i = 0

    def next(self):
        it = self.items[self.i]
        self.i = (self.i + 1) % len(self.items)
        return it


def _t5_bucket(dist):
    max_exact = 16
    n = np.maximum(dist, 0)
    nf = np.maximum(n, 1).astype(np.float32)
    large = max_exact + (np.log(nf / np.float32(max_exact)) / np.float32(math.log(128 / max_exact))
                         * np.float32(32 - max_exact)).astype(np.int32)
    large = np.minimum(large, 31)
    return np.where(n < max_exact, n, large)


def _host_consts():
    bf = ml_dtypes.bfloat16
    k = np.arange(128)[:, None]
    q = np.arange(128)[None, :]
    cb = np.zeros((128, 6, 128), np.float32)
    cb[:, 0] = np.eye(128)
    cb[:, 1] = -1.0 * (k >= q)
    cb[:, 2] = -1.0
    cb[:, 3] = 1.0
    cb[:, 4] = (k < q)
    cb[:, 5] = (k <= q)
    cbf = cb.astype(bf)
    invf = np.power(np.float32(10000.0), -np.arange(16, dtype=np.float32) / np.float32(16)).astype(np.float32)
    cf = np.zeros((128, 64 + 256), np.float32)
    cf[:, 0:16] = invf
    cf[:, 16:32] = invf
    cf[:, 32:48] = 0.0
    cf[:, 48:64] = np.float32(math.pi / 2)
    m = np.zeros((128, 2, 128), np.float32)
    m[:, 0] = (k > q)
    m[:, 1] = (k <= q)
    cf[:, 64:320] = m.reshape(128, 256)
    bidx = np.zeros((128, 2, 128), np.int64)
    bidx[:, 0] = _t5_bucket(128 + q - k)
    bidx[:, 1] = _t5_bucket(q - k)
    return cbf, cf, bidx


DBG_CUT = [0]
SCHEDULE = [True]
PRIO_CP = [False]
WARM_FILL = [0.9]
WARM_COST = 144.0
WARM_BANK = [6]
DEPTHS = {"E": 3, "SP": 2, "TM": 2, "W": 2}
ARENA_KB = [109]
EXPERIMENT = [False]


def build_nc(SEQ=2048, DEPTH=2, NSEQ=2, dbg=None, stop_after=None):
    NT = SEQ // 128
    NSB = SEQ // 512
    nc = bass.Bass("TRN2", target_bir_lowering=False)

    def din(name, shape, dt=F32):
        return nc.dram_tensor(name, list(shape), dt, kind="ExternalInput").ap()

    x_d = din("x", [NSEQ, SEQ, D])
    cT_d = din("cT", [128, 8, NSEQ])
    posT_d = din("posT", [NSEQ, 128, NT], I32)
    relg_d = din("relg", [128, 6, 256])
    norm1_d = din("norm1_g", [DEPTH, D])
    norm2_d = din("norm2_g", [DEPTH, D])
    w_ada_d = din("w_ada", [DEPTH, D, 6 * D])
    b_ada_d = din("b_ada", [DEPTH, 6 * D])
    w_in_d = din("w_in", [DEPTH, D, D_IN])
    cqg_d = din("mla_cq_g", [DEPTH, 256])
    w_uq_d = din("w_uq", [DEPTH, 256, 576])
    ckvg_d = din("mla_ckv_g", [DEPTH, 128])
    w_ukv_d = din("w_ukv", [DEPTH, 128, 768])
    qng_d = din("mla_qn_g", [DEPTH, 96])
    kng_d = din("mla_kn_g", [DEPTH, 96])
    swg_d = din("sw_g", [DEPTH, 128])
    sink_d = din("sinkT", [DEPTH, 128, 3])
    w_out_d = din("w_out", [DEPTH, D, D])
    w_up_d = din("w_up", [DEPTH, D, 2 * D_FF])
    cw_d = din("cwT", [DEPTH, 128, 44, 3])
    cbias_d = din("cbT", [DEPTH, 128, 44])
    w_down_d = din("w_down", [DEPTH, D_FF, D])
    cbf_d = din("cbf", [128, 6, 128], BF16)
    cf_d = din("cf", [128, 320])
    out_d = nc.dram_tensor("out", [NSEQ, SEQ, D], F32, kind="ExternalOutput").ap()

    def dscr(name, shape, dt):
        return nc.dram_tensor(name, list(shape), dt).ap()
    wb_ada = dscr("wb_ada", [DEPTH, D, 6 * D], BF16)
    wb_in = dscr("wb_in", [DEPTH, D, D_IN], BF16)
    wb_uq = dscr("wb_uq", [DEPTH, 256, 576], BF16)
    wb_ukv = dscr("wb_ukv", [DEPTH, 128, 768], BF16)
    wb_out = dscr("wb_out", [DEPTH, D, D], BF16)
    wb_up = dscr("wb_up_t", [DEPTH, NFC, 128, 8, 256], BF16)
    wb_down = dscr("wb_down", [DEPTH, D_FF, D], BF16)
    mods_d = dscr("mods", [DEPTH, NSEQ, 6 * D], F32)
    dbg_outs = {}
    if dbg:
        for name, shape in dbg.items():
            dbg_outs[name] = nc.dram_tensor("dbg_" + name, list(shape), F32, kind="ExternalOutput").ap()

    S = Sched(nc)
    with contextlib.ExitStack() as st:
        def sb(name, shape, dt):
            return st.enter_context(nc.sbuf_tensor(name, list(shape), dt))

        sems = {e: st.enter_context(nc.semaphore("s_" + e)) for e in Sched.ENGS}
        dsems = [st.enter_context(nc.semaphore("d%d" % i)) for i in range(N_DMA_SEMS)]

        PS = []
        for i in range(8):
            t = st.enter_context(nc.psum_tensor("ps%d" % i, [128, 512], F32))
            PS.append((t, Res("ps%d" % i, excl=True)))
        rot4 = Rot(PS[0:4] + PS[6:8])

        X = sb("X", [128, NT, D], F32 if not EXPERIMENT[0] else BF16)
        rX = [Res("X%d" % t) for t in range(NT)]
        CB = sb("CB", [128, 6, 128], BF16); rCB = Res("CB")
        CF = sb("CF", [128, 320], F32); rCF = Res("CF")
        ident = CB[:, 0, :]
        negU = CB[:, 1, :]
        negOnes = CB[:, 2, :]
        ones = CB[:, 3, :]
        maskS = CB[:, 4, :]
        maskC = CB[:, 5, :]
        M0 = sb("M0", [128, D], F32); rM0 = Res("M0")
        M1 = sb("M1", [128, D], F32); rM1 = Res("M1")
        G_cq = sb("G_cq", [128, 256], F32)
        G_ckv = sb("G_ckv", [128, 128], F32)
        G_q = sb("G_q", [128, 96], F32)
        G_k = sb("G_k", [128, 96], F32)
        G_sw = sb("G_sw", [128, 2, 64], F32)
        ESINK = sb("ESINK", [128, 3], F32)
        CW = sb("CW", [128, 44, 3], F32)
        CBI = sb("CBI", [128, 44], F32)
        rG = Res("gains")
        hT = sb("hT", [128, 8, 512], BF16); rhT = Res("hT")
        xn = sb("xn", [128, D], F32); rxn = Res("xn")
        hbf = sb("hbf", [128, D], BF16); rhbf = Res("hbf")
        SC = sb("sincos", [128, NT, 32], F32); rSC = Res("sincos")
        stat = sb("stat", [128, 64], F32); rstat = Res("stat"); rstat1 = Res("stat1"); rstat4 = Res("stat4"); rstat5 = Res("stat5"); rstat8 = Res("stat8"); rstat16 = Res("stat16")
        ESWt = sb("ESW", [128, 6, 2, 128], BF16); rESW = Res("ESW")
        ZT = sb("zeros", [128, 384], BF16)
        ESW = ESWt[:]
        ARENA_BYTES = ARENA_KB[0] * 1024
        ARENA = sb("arena", [128, ARENA_BYTES // 2], BF16)

        class Carve:
            def __init__(self, start=0):
                self.off = start

            def bf(self, shape):
                n = int(np.prod(shape[1:]))
                a = ARENA[:, self.off // 2: self.off // 2 + n]
                self.off += 2 * n
                assert self.off <= ARENA_BYTES, self.off
                return self._shape(a, shape)

            def f32(self, shape):
                n = int(np.prod(shape[1:]))
                self.off = (self.off + 3) // 4 * 4
                a = ARENA[:, self.off // 2: self.off // 2 + 2 * n].bitcast(F32)
                self.off += 4 * n
                assert self.off <= ARENA_BYTES, self.off
                return self._shape(a, shape)

            @staticmethod
            def _shape(a, shape):
                if shape[0] < 128:
                    a = a[0:shape[0]]
                if len(shape) == 2:
                    return a
                if len(shape) == 3:
                    return a.rearrange("p (a b) -> p a b", a=shape[1])
                if len(shape) == 4:
                    return a.rearrange("p (a b c) -> p a b c", a=shape[1], b=shape[2])
                raise ValueError

        cv0 = Carve()
        mixT = cv0.bf([128, 8, SEQ])
        rmix = [Res("mix%d" % j) for j in range(NSB)]
        ATT_BASE = cv0.off

        sp_q = "sp"

        def load(dst, src, wres, rres=(), free=False):
            S.dma(sp_q, lambda e: e.dma_start(out=dst, in_=src), reads=list(rres), writes=list(wres), free=free)

        def rstd_from_ss(col0, ncols, n, rs):
            a = stat[:, col0:col0 + ncols]
            S.op("act", lambda e: e.activation(a, a, AF.Ln, bias=EPS, scale=1.0 / n), reads=[rs], writes=[rs])
            S.op("act", lambda e: e.activation(a, a, AF.Exp, scale=-0.5), reads=[rs], writes=[rs])

        def transpose_to(ps_ap, src_ap, res_src, res_ps):
            S.op("pe", lambda e: e.matmul(ps_ap, lhsT=src_ap, rhs=ident, start=True, stop=True),
                 reads=[res_src, rCB], writes=[res_ps])

        def norm_and_transpose(j, bufs=None, pre=None):
            if bufs is None:
                bufs = (hT, rhT, xn[:], rxn, hbf[:], rhbf, 0, rstat)
            hT_, rhT_, xn_, rxn_, hbf_, rhbf_, sc_, rst_ = bufs
            for t in range(4):
                tg = 4 * j + t
                xt = X[:, tg, :]
                if pre is None:
                    S.op("act", lambda e, xt=xt: e.activation(hbf_, xt, AF.Square, accum_out=stat[:, sc_:sc_ + 1]),
                         reads=[rX[tg]], writes=[rhbf_, rst_])
                    rstd_from_ss(sc_, 1, D, rst_)
                    rcol, rres = stat[:, sc_:sc_ + 1], rst_
                else:
                    rcol, rres = pre[0][:, tg:tg + 1], pre[1]
                S.op("dve", lambda e, xt=xt, rcol=rcol: e.scalar_tensor_tensor(xn_, xt, rcol, M0[:], ALU.mult, ALU.mult),
                     reads=[rX[tg], rres, rM0], writes=[rxn_])
                S.op("dve", lambda e: e.tensor_tensor(hbf_, xn_, M1[:], ALU.add), reads=[rxn_, rM1], writes=[rhbf_])
                for g in range(2):
                    pt, rp = rot4.next()
                    for c in range(4):
                        kc = 4 * g + c
                        transpose_to(pt[:, c * 128:(c + 1) * 128], hbf_[:, kc * 128:(kc + 1) * 128], rhbf_, rp)
                    S.op("act", lambda e, pt=pt, g=g, t=t: e.copy(hT_[:, 4 * g:4 * g + 4, t * 128:(t + 1) * 128],
                                                                 pt[:].rearrange("p (c n) -> p c n", c=4)),
                         reads=[rp], writes=[rhT_])
            return hT_, rhT_

        def load_mods(l, s, which, dst, rdst):
            load(dst[:], mods_d[l, s, which * D:(which + 1) * D].partition_broadcast(128), [rdst], [r_mods])

        def setup_norm_mods(l, s, sub):
            gsrc = norm1_d if sub == 0 else norm2_d
            load(M1[:], gsrc[l, :].partition_broadcast(128), [rM1])
            load_mods(l, s, 3 * sub + 1, M0, rM0)
            S.op("dve", lambda e: e.scalar_tensor_tensor(M0[:], M0[:], 1.0, M1[:], ALU.add, ALU.mult),
                 reads=[rM0, rM1], writes=[rM0])
            load_mods(l, s, 3 * sub + 0, M1, rM1)

        def residual_update(ps_ap, rp, tg, half, gate, rgate, tmp, rtmp):
            S.op("dve", lambda e: e.tensor_tensor(tmp, ps_ap, gate[:, half * 512:(half + 1) * 512], ALU.mult),
                 reads=[rp, rgate], writes=[rtmp])
            xa = X[:, tg, half * 512:(half + 1) * 512]
            S.op("dve", lambda e: e.tensor_tensor(xa, xa, tmp, ALU.add), reads=[rtmp, rX[tg]], writes=[rX[tg]])

        load(CB[:], cbf_d, [rCB])
        load(CF[:], cf_d, [rCF])
        rZT = Res("zeros")
        S.warm_dep = S.op("dve", lambda e: e.memset(ZT[:], 0.0), writes=[rZT])
        if WARM_FILL[0] > 0:
            S.warm_fn = lambda e: e.matmul(PS[WARM_BANK[0]][0][:, 0:256], lhsT=ZT[:, 0:128], rhs=ZT[:, 128:384], start=True, stop=True)
        r_wb = {}

        def convert(name, dst, src, rows, l, after=()):
            r = Res("wb_%s_%d" % (name, l))
            r_wb[(name, l)] = r
            for r0 in range(0, rows, 128):
                nr = min(128, rows - r0)
                S.dma("pool", lambda e, r0=r0, nr=nr: e.dma_start(out=dst[l, r0:r0 + nr, :], in_=src[l, r0:r0 + nr, :]),
                      reads=list(after), writes=[r], free=True)

        def convert_up(l, after=()):
            r = Res("wb_up_%d" % l)
            r_wb[("up", l)] = r
            for kc in range(8):
                for gv in range(2):
                    for i0 in range(0, NFC, 11):
                        ni = min(11, NFC - i0)
                        c0 = gv * D_FF + i0 * 128
                        src = w_up_d[l, kc * 128:(kc + 1) * 128, c0:c0 + ni * 128].rearrange("p (i n) -> p i n", i=ni)
                        dst = wb_up[l, i0:i0 + ni, :, kc, gv * 128:(gv + 1) * 128].rearrange("i p n -> p i n")
                        S.dma("pool", lambda e, dst=dst, src=src: e.dma_start(out=dst, in_=src),
                              reads=list(after), writes=[r], free=True)

        convert("in", wb_in, w_in_d, D, 0)
        convert("uq", wb_uq, w_uq_d, 256, 0)
        convert("ukv", wb_ukv, w_ukv_d, 128, 0)

        def convert_rest():
            for l in range(DEPTH):
                if l > 0:
                    convert("ada", wb_ada, w_ada_d, D, l, after=[r_mods])
                    convert("in", wb_in, w_in_d, D, l, after=[r_mods])
                    convert("uq", wb_uq, w_uq_d, 256, l, after=[r_mods])
                    convert("ukv", wb_ukv, w_ukv_d, 128, l, after=[r_mods])
                convert("out", wb_out, w_out_d, D, l, after=[r_mods])
                convert_up(l, after=[r_mods])
                convert("down", wb_down, w_down_d, D_FF, l, after=[r_mods])

        r_mods = Res("mods")

        def compute_mods(l):
            cvp = Carve(ATT_BASE)
            cT = cvp.f32([128, 8, NSEQ])
            scT = cvp.bf([128, 8, NSEQ])
            WA_buf = [(cvp.bf([128, 8, 512]), Res("wada%d" % i)) for i in range(3)]
            MR = Rot([(cvp.f32([NSEQ, 512]), Res("modrow%d" % i)) for i in range(2)])
            BA = Rot([(cvp.f32([NSEQ, 512]), Res("badd%d" % i)) for i in range(2)])
            r_pro = Res("pro")
            load(cT, cT_d, [r_pro])
            S.op("act", lambda e: e.activation(scT, cT, AF.Silu), reads=[r_pro], writes=[r_pro])
            wrot = Rot(WA_buf)
            for n in range(12):
                wbuf, rw = wrot.next()
                badd, rba = BA.next()
                modrow, rmr = MR.next()
                load(badd, b_ada_d[l, n * 512:(n + 1) * 512].partition_broadcast(NSEQ), [rba], [])
                if l == 0:
                    S.dma("pool", lambda e, wbuf=wbuf, n=n: e.dma_start(
                        out=wbuf, in_=w_ada_d[l, :, n * 512:(n + 1) * 512].rearrange("(c p) n -> p c n", p=128)), writes=[rw])
                else:
                    load(wbuf, wb_ada[l, :, n * 512:(n + 1) * 512].rearrange("(c p) n -> p c n", p=128), [rw],
                         [r_wb[("ada", l)]])
                pt, rp = rot4.next()
                for kc in range(8):
                    S.op("pe", lambda e, pt=pt, kc=kc, wbuf=wbuf: e.matmul(
                        pt[0:NSEQ, :], lhsT=scT[:, kc, :], rhs=wbuf[:, kc, :], start=(kc == 0), stop=(kc == 7)),
                        reads=[r_pro, rw], writes=[rp])
                S.op("dve", lambda e, pt=pt, modrow=modrow, badd=badd: e.tensor_tensor(modrow, pt[0:NSEQ, :], badd, ALU.add),
                     reads=[rp, rba], writes=[rmr])
                S.dma(sp_q, lambda e, l=l, n=n, modrow=modrow: e.dma_start(out=mods_d[l, :, n * 512:(n + 1) * 512], in_=modrow),
                      reads=[rmr], writes=[r_mods])
            S.barrier()

        def rope_tables(s):
            cv = Carve(ATT_BASE)
            posi = cv.f32([128, NT]).bitcast(I32)
            posf = cv.f32([128, NT])
            ang = cv.f32([128, NT, 32])
            ki = cv.f32([128, NT, 32]).bitcast(I32)
            kf = cv.f32([128, NT, 32])
            r = Res("ropetmp")
            load(posi, posT_d[s], [r])
            S.op("dve", lambda e: e.tensor_copy(posf, posi), reads=[r], writes=[r])
            S.op("dve", lambda e: e.tensor_tensor(ang, posf.unsqueeze(2).to_broadcast([128, NT, 32]),
                                                  CF[:, 0:32].unsqueeze(1).to_broadcast([128, NT, 32]), ALU.mult),
                 reads=[r, rCF], writes=[r])
            S.op("dve", lambda e: e.tensor_tensor(ang, ang, CF[:, 32:64].unsqueeze(1).to_broadcast([128, NT, 32]), ALU.add),
                 reads=[r, rCF], writes=[r])
            S.op("dve", lambda e: e.tensor_scalar(kf, ang, 1.0 / (2 * math.pi), None, ALU.mult), reads=[r], writes=[r])
            S.op("dve", lambda e: e.tensor_copy(ki, kf), reads=[r], writes=[r])
            S.op("dve", lambda e: e.tensor_copy(kf, ki), reads=[r], writes=[r])
            S.op("dve", lambda e: e.scalar_tensor_tensor(ang, kf, -2 * math.pi, ang, ALU.mult, ALU.add), reads=[r], writes=[r])
            S.op("dve", lambda e: e.tensor_scalar(kf, ang, math.pi, -2 * math.pi, ALU.is_gt, ALU.mult), reads=[r], writes=[r])
            S.op("dve", lambda e: e.tensor_tensor(ang, ang, kf, ALU.add), reads=[r], writes=[r])
            S.op("dve", lambda e: e.tensor_scalar(kf, ang, -math.pi, 2 * math.pi, ALU.is_lt, ALU.mult), reads=[r], writes=[r])
            S.op("dve", lambda e: e.tensor_tensor(ang, ang, kf, ALU.add), reads=[r], writes=[r])
            S.op("act", lambda e: e.activation(SC[:], ang, AF.Sin), reads=[r], writes=[rSC])
            S.barrier()

        def pass_A(l, s, cv):
            W = cv.bf([128, 8, 768]); rW = Res("WA")
            QT = cv.bf([128, 2, 512]); rQT = Res("QTsb")
            KT = cv.bf([128, 2, SEQ]); rKT = [Res("KTsb%d" % j) for j in range(NSB)]
            V = cv.bf([128, NT, 256]); rV = [Res("Vsb%d" % j) for j in range(NSB)]
            Eb = Rot([(cv.f32([128, 512]), Res("E%d" % i)) for i in range(DEPTHS["E"])])
            SPb = Rot([(cv.bf([128, 512]), Res("SP%d" % i)) for i in range(DEPTHS["SP"])])
            TMb = Rot([(cv.f32([128, 512]), Res("TM%d" % i)) for i in range(DEPTHS["TM"])])
            Wb = Rot([(cv.bf([128, 512]), Res("Wt%d" % i)) for i in range(DEPTHS["W"])])
            SPS = cv.bf([128, 512]); rSPS = Res("SPS")
            load(W, wb_in[l, :, 0:768].rearrange("(c p) n -> p c n", p=128), [rW], [r_wb[("in", l)]])
            Oacc, rO = PS[4]

            def body(j):
                for which in range(2):
                    for p in range(2):
                        pt, rp = rot4.next()
                        c0 = which * 256 + p * 128
                        for kc in range(8):
                            S.op("pe", lambda e, pt=pt, kc=kc, c0=c0: e.matmul(
                                pt[:], lhsT=W[:, kc, c0:c0 + 128], rhs=hT[:, kc, :], start=(kc == 0), stop=(kc == 7)),
                                reads=[rW, rhT], writes=[rp])
                        if which == 0:
                            S.op("act", lambda e, pt=pt, p=p: e.copy(QT[:, p, :], pt[:]), reads=[rp], writes=[rQT])
                        else:
                            S.op("act", lambda e, pt=pt, p=p, j=j: e.copy(KT[:, p, j * 512:(j + 1) * 512], pt[:]),
                                 reads=[rp], writes=[rKT[j]])
                for t in range(4):
                    pt, rp = rot4.next()
                    for kc in range(8):
                        S.op("pe", lambda e, pt=pt, kc=kc, t=t: e.matmul(
                            pt[:, 0:256], lhsT=hT[:, kc, t * 128:(t + 1) * 128], rhs=W[:, kc, 512:768],
                            start=(kc == 0), stop=(kc == 7)), reads=[rW, rhT], writes=[rp])
                    S.op("act", lambda e, pt=pt, t=t, j=j: e.copy(V[:, 4 * j + t, :], pt[:, 0:256]),
                         reads=[rp], writes=[rV[j]])
                yield "proj"
                for p in range(2):
                    for hh in range(2):
                        h = 2 * p + hh
                        b0 = 64 * hh
                        S.op("dve", lambda e: e.memset(SPS, 0.0), writes=[rSPS])
                        nkb = 4 * j + 4
                        for kb in range(nkb - 1, -1, -1):
                            first = (kb == nkb - 1)
                            qlo = max(0, kb - 4 * j)
                            c0 = qlo * 128
                            jk = kb // 4
                            Z, rZ = rot4.next()
                            S.op("pe", lambda e, Z=Z, b0=b0, p=p, kb=kb, c0=c0: e.matmul(
                                Z[:, c0:512], lhsT=KT[b0:b0 + 64, p, kb * 128:(kb + 1) * 128], rhs=QT[b0:b0 + 64, p, c0:512],
                                start=True, stop=True), reads=[rKT[jk], rQT], writes=[rZ])
                            E, rE = Eb.next()
                            S.op("act", lambda e, E=E, Z=Z, c0=c0: e.activation(E[:, c0:512], Z[:, c0:512], AF.Exp, scale=0.125),
                                 reads=[rZ], writes=[rE])
                            SP, rSP = SPb.next()
                            S.op("act", lambda e, E=E, SP=SP, c0=c0: e.activation(SP[:, c0:512], E[:, c0:512], AF.Ln, bias=1.0),
                                 reads=[rE], writes=[rSP])
                            if kb >= 4 * j:
                                S.op("dve", lambda e, SP=SP, c0=c0: e.tensor_tensor(SP[:, c0:c0 + 128], SP[:, c0:c0 + 128], maskS, ALU.mult),
                                     reads=[rSP, rCB], writes=[rSP])
                            C, rC = rot4.next()
                            S.op("pe", lambda e, C=C, SP=SP, c0=c0, first=first: e.matmul(C[:, c0:512], lhsT=negU, rhs=SP[:, c0:512], start=True, stop=first),
                                 reads=[rSP, rCB], writes=[rC])
                            if not first:
                                S.op("pe", lambda e, C=C, c0=c0: e.matmul(C[:, c0:512], lhsT=negOnes, rhs=SPS[:, c0:512], start=False, stop=True),
                                     reads=[rSPS, rCB], writes=[rC])
                            G, rG_ = TMb.next()
                            S.op("act", lambda e, G=G, C=C, c0=c0: e.activation(G[:, c0:512], C[:, c0:512], AF.Exp),
                                 reads=[rC], writes=[rG_])
                            if kb > 0:
                                S.op("dve", lambda e, SP=SP, c0=c0: e.tensor_tensor(SPS[:, c0:512], SPS[:, c0:512], SP[:, c0:512], ALU.add),
                                     reads=[rSP, rSPS], writes=[rSPS])
                            Wt, rWt = Wb.next()
                            if first and c0 > 0:
                                S.op("dve", lambda e, Wt=Wt, c0=c0: e.memset(Wt[:, 0:c0], 0.0), writes=[rWt])
                            S.op("dve", lambda e, Wt=Wt, E=E, G=G, c0=c0: e.tensor_tensor(Wt[:, c0:512], E[:, c0:512], G[:, c0:512], ALU.mult),
                                 reads=[rE, rG_], writes=[rWt])
                            if kb >= 4 * j:
                                S.op("dve", lambda e, Wt=Wt, c0=c0: e.tensor_tensor(Wt[:, c0:c0 + 128], Wt[:, c0:c0 + 128], maskS, ALU.mult),
                                     reads=[rWt, rCB], writes=[rWt])
                            pc0 = 0 if first else c0
                            S.op("pe", lambda e, Wt=Wt, b0=b0, kb=kb, h=h, pc0=pc0, first=first: e.matmul(
                                Oacc[b0:b0 + 64, pc0:512], lhsT=V[:, kb, h * 64:(h + 1) * 64], rhs=Wt[:, pc0:512],
                                start=first, stop=(kb == 0)), reads=[rWt, rV[jk]], writes=[rO])
                            yield "it"
                    S.op("act", lambda e, p=p, j=j: e.copy(mixT[:, p, j * 512:(j + 1) * 512], Oacc[:]), reads=[rO], writes=[rmix[j]])
            return body

        def pass_B(l, s):
            cv = Carve(ATT_BASE)
            W = cv.bf([128, 8, 416]); rW = Res("WB")
            WUQ = cv.bf([128, 2, 576]); WUKV = cv.bf([128, 768])
            QT = cv.bf([128, 6, 512]); rQT = Res("QTm")
            KT = cv.bf([128, 6, SEQ]); rKT = [Res("KTm%d" % j) for j in range(NSB)]
            V = cv.bf([128, NT, 384]); rV = [Res("Vm%d" % j) for j in range(NSB)]
            lat = cv.f32([128, 416]); rlat = Res("lat")
            cqn = cv.bf([128, 384]); rcqn = Res("cqn")
            latT = cv.bf([128, 3, 128]); rlatT = Res("latT")
            qf = cv.f32([128, 6, 96]); rqf = Res("qf")
            kf = cv.f32([128, 6, 96]); rkf = Res("kf")
            sq = cv.f32([128, 6, 96]); rsq = Res("sq")
            rt = cv.f32([128, 6, 4, 16]); rrt = Res("rt")
            q16 = cv.bf([128, 6, 96]); rq16 = Res("q16")
            k16 = cv.bf([128, 6, 96]); rk16 = Res("k16")
            Pb = Rot([(cv.bf([128, 512]), Res("P%d" % i)) for i in range(3)])
            rec = cv.f32([128, 512]); rrec = Res("rec")
            load(W, wb_in[l, :, 768:1184].rearrange("(c p) n -> p c n", p=128), [rW], [r_wb[("in", l)]])
            load(WUQ, wb_uq[l].rearrange("(c p) n -> p c n", p=128), [rW], [r_wb[("uq", l)]])
            load(WUKV, wb_ukv[l], [rW], [r_wb[("ukv", l)]])
            load(G_cq[:], cqg_d[l, :].partition_broadcast(128), [rG])
            load(G_ckv[:], ckvg_d[l, :].partition_broadcast(128), [rG])
            load(G_q[:], qng_d[l, :].partition_broadcast(128), [rG])
            load(G_k[:], kng_d[l, :].partition_broadcast(128), [rG])
            Oacc, rO = PS[4]
            Dacc, rD = PS[5]

            def qk_norm_rope(src, rsrc, gain, dst16, rdst, tg):
                S.op("dve", lambda e: e.tensor_tensor(sq, src, src, ALU.mult), reads=[rsrc], writes=[rsq])
                S.op("dve", lambda e: e.tensor_reduce(stat[:, 8:14], sq, axis=AX.X, op=ALU.add), reads=[rsq], writes=[rstat8])
                rstd_from_ss(8, 6, 96, rstat8)
                S.op("dve", lambda e: e.tensor_tensor(src, src, stat[:, 8:14].unsqueeze(2).to_broadcast([128, 6, 96]), ALU.mult),
                     reads=[rsrc, rstat8], writes=[rsrc])
                S.op("dve", lambda e: e.tensor_tensor(src, src, gain[:].unsqueeze(1).to_broadcast([128, 6, 96]), ALU.mult),
                     reads=[rsrc, rG], writes=[rsrc])
                S.op("dve", lambda e: e.tensor_copy(dst16[:, :, 0:64], src[:, :, 0:64]), reads=[rsrc], writes=[rdst])
                sin = SC[:, tg, 0:16].unsqueeze(1).to_broadcast([128, 6, 16])
                cos = SC[:, tg, 16:32].unsqueeze(1).to_broadcast([128, 6, 16])
                x1 = src[:, :, 64:80]
                x2 = src[:, :, 80:96]
                S.op("dve", lambda e: e.tensor_tensor(rt[:, :, 0, :], x1, cos, ALU.mult), reads=[rsrc, rSC], writes=[rrt])
                S.op("dve", lambda e: e.tensor_tensor(rt[:, :, 1, :], x2, sin, ALU.mult), reads=[rsrc, rSC], writes=[rrt])
                S.op("dve", lambda e: e.tensor_tensor(rt[:, :, 2, :], x1, sin, ALU.mult), reads=[rsrc, rSC], writes=[rrt])
                S.op("dve", lambda e: e.tensor_tensor(rt[:, :, 3, :], x2, cos, ALU.mult), reads=[rsrc, rSC], writes=[rrt])
                S.op("dve", lambda e: e.tensor_tensor(dst16[:, :, 64:80], rt[:, :, 0, :], rt[:, :, 1, :], ALU.subtract),
                     reads=[rrt], writes=[rdst])
                S.op("dve", lambda e: e.tensor_tensor(dst16[:, :, 80:96], rt[:, :, 2, :], rt[:, :, 3, :], ALU.add),
                     reads=[rrt], writes=[rdst])

            norm_and_transpose(0)
            for j in range(NSB):
                for t in range(4):
                    tg = 4 * j + t
                    pt, rp = rot4.next()
                    for kc in range(8):
                        S.op("pe", lambda e, pt=pt, kc=kc, t=t: e.matmul(
                            pt[:, 0:416], lhsT=hT[:, kc, t * 128:(t + 1) * 128], rhs=W[:, kc, :],
                            start=(kc == 0), stop=(kc == 7)), reads=[rW, rhT], writes=[rp])
                    S.op("act", lambda e, pt=pt: e.copy(lat, pt[:, 0:416]), reads=[rp], writes=[rlat])
                    if DBG_CUT[0] == 1:
                        S.barrier(); return
                    S.op("act", lambda e: e.activation(sq[:, 0:3, :].rearrange("p a b -> p (a b)")[:, 0:256], lat[:, 0:256], AF.Square,
                                                       accum_out=stat[:, 4:5]), reads=[rlat], writes=[rsq, rstat4])
                    rstd_from_ss(4, 1, 256, rstat4)
                    S.op("act", lambda e: e.activation(sq[:, 0:3, :].rearrange("p a b -> p (a b)")[:, 0:128], lat[:, 256:384], AF.Square,
                                                       accum_out=stat[:, 5:6]), reads=[rlat], writes=[rsq, rstat5])
                    rstd_from_ss(5, 1, 128, rstat5)
                    S.op("dve", lambda e: e.scalar_tensor_tensor(cqn[:, 0:256], lat[:, 0:256], stat[:, 4:5], G_cq[:], ALU.mult, ALU.mult),
                         reads=[rlat, rstat4, rG], writes=[rcqn])
                    S.op("dve", lambda e: e.scalar_tensor_tensor(cqn[:, 256:384], lat[:, 256:384], stat[:, 5:6], G_ckv[:], ALU.mult, ALU.mult),
                         reads=[rlat, rstat5, rG], writes=[rcqn])
                    pt, rp = rot4.next()
                    for c in range(3):
                        transpose_to(pt[:, c * 128:(c + 1) * 128], cqn[:, c * 128:(c + 1) * 128], rcqn, rp)
                    S.op("act", lambda e, pt=pt: e.copy(latT, pt[:, 0:384].rearrange("p (c n) -> p c n", c=3)), reads=[rp], writes=[rlatT])
                    if DBG_CUT[0] == 2:
                        S.barrier(); return
                    for half in range(2):
                        pt, rp = rot4.next()
                        for c in range(2):
                            S.op("pe", lambda e, pt=pt, c=c, half=half: e.matmul(
                                pt[:, 0:288], lhsT=latT[:, c, :], rhs=WUQ[:, c, half * 288:(half + 1) * 288],
                                start=(c == 0), stop=(c == 1)), reads=[rlatT, rW], writes=[rp])
                        S.op("act", lambda e, pt=pt, half=half: e.copy(qf[:, 3 * half:3 * half + 3, :],
                                                                     pt[:, 0:288].rearrange("p (a b) -> p a b", a=3)),
                             reads=[rp], writes=[rqf])
                    if DBG_CUT[0] == 6:
                        S.barrier(); return
                    for half in range(2):
                        pt, rp = rot4.next()
                        S.op("pe", lambda e, pt=pt, half=half: e.matmul(
                            pt[:, 0:384], lhsT=latT[:, 2, :], rhs=WUKV[:, half * 384:(half + 1) * 384], start=True, stop=True),
                            reads=[rlatT, rW], writes=[rp])
                        pv = pt[:, 0:384].rearrange("p (a b) -> p a b", a=3)
                        S.op("act", lambda e, pv=pv, half=half: e.copy(kf[:, 3 * half:3 * half + 3, 0:64], pv[:, :, 0:64]),
                             reads=[rp], writes=[rkf])
                        if DBG_CUT[0] == 7:
                            continue
                        S.op("dve", lambda e, pv=pv, half=half, tg=tg: e.tensor_copy(
                            V[:, tg, half * 192:(half + 1) * 192].rearrange("p (a b) -> p a b", a=3), pv[:, :, 64:128]),
                            reads=[rp], writes=[rV[j]])
                    if DBG_CUT[0] in (7, 8):
                        S.barrier(); return
                    S.op("dve", lambda e: e.tensor_copy(kf[:, :, 64:96], lat[:, 384:416].unsqueeze(1).to_broadcast([128, 6, 32])),
                         reads=[rlat], writes=[rkf])
                    if DBG_CUT[0] == 3:
                        S.barrier(); return
                    qk_norm_rope(qf, rqf, G_q, q16, rq16, tg)
                    if DBG_CUT[0] == 4:
                        S.barrier(); return
                    qk_norm_rope(kf, rkf, G_k, k16, rk16, tg)
                    for (src16, rsrc16, isq) in ((q16, rq16, True), (k16, rk16, False)):
                        for (h0, nh) in ((0, 4), (4, 2)):
                            pt, rp = rot4.next()
                            for hh in range(nh):
                                transpose_to(pt[0:96, hh * 128:(hh + 1) * 128], src16[:, h0 + hh, :], rsrc16, rp)
                            pv = pt[0:96, 0:nh * 128].rearrange("p (a b) -> p a b", a=nh)
                            if isq:
                                S.op("act", lambda e, pv=pv, h0=h0, nh=nh, t=t: e.copy(QT[0:96, h0:h0 + nh, t * 128:(t + 1) * 128], pv),
                                     reads=[rp], writes=[rQT])
                            else:
                                S.op("act", lambda e, pv=pv, h0=h0, nh=nh, tg=tg: e.copy(KT[0:96, h0:h0 + nh, tg * 128:(tg + 1) * 128], pv),
                                     reads=[rp], writes=[rKT[j]])
                if j + 1 < NSB:
                    norm_and_transpose(j + 1)
                sc = 1.0 / math.sqrt(96.0)
                for p in range(3):
                    for hh in range(2):
                        h = 2 * p + hh
                        b0 = 64 * hh
                        nkb = 4 * j + 4
                        for kb in range(nkb):
                            qlo = max(0, kb - 4 * j)
                            c0 = qlo * 128
                            jk = kb // 4
                            Z, rZ = rot4.next()
                            S.op("pe", lambda e, Z=Z, h=h, kb=kb, c0=c0: e.matmul(
                                Z[:, c0:512], lhsT=KT[0:96, h, kb * 128:(kb + 1) * 128], rhs=QT[0:96, h, c0:512],
                                start=True, stop=True), reads=[rKT[jk], rQT], writes=[rZ])
                            P, rP = Pb.next()
                            S.op("act", lambda e, P=P, Z=Z, c0=c0: e.activation(P[:, c0:512], Z[:, c0:512], AF.Exp, scale=sc),
                                 reads=[rZ], writes=[rP])
                            if kb >= 4 * j:
                                S.op("dve", lambda e, P=P, c0=c0: e.tensor_tensor(P[:, c0:c0 + 128], P[:, c0:c0 + 128], maskC, ALU.mult),
                                     reads=[rP, rCB], writes=[rP])
                            S.op("pe", lambda e, P=P, b0=b0, kb=kb, h=h, c0=c0, nkb=nkb: e.matmul(
                                Oacc[b0:b0 + 64, c0:512], lhsT=V[:, kb, h * 64:(h + 1) * 64], rhs=P[:, c0:512],
                                start=(kb == 0), stop=(kb == nkb - 1)), reads=[rP, rV[jk]], writes=[rO])
                            S.op("pe", lambda e, P=P, b0=b0, kb=kb, c0=c0, nkb=nkb: e.matmul(
                                Dacc[b0:b0 + 64, c0:512], lhsT=ones[:, 0:64], rhs=P[:, c0:512],
                                start=(kb == 0), stop=(kb == nkb - 1)), reads=[rP, rCB], writes=[rD])
                    S.op("dve", lambda e: e.reciprocal(rec, Dacc[:]), reads=[rD], writes=[rrec])
                    S.op("dve", lambda e, p=p, j=j: e.tensor_tensor(mixT[:, 2 + p, j * 512:(j + 1) * 512], Oacc[:], rec, ALU.mult),
                         reads=[rO, rrec], writes=[rmix[j]])
            S.barrier()

        def pass_C(l, s, cv):
            W = cv.bf([128, 8, 640]); rW = Res("WC")
            QT = cv.bf([128, 3, 512]); rQT = Res("QTs")
            KT = cv.bf([128, SEQ]); rKT = [Res("KTs%d" % j) for j in range(NSB)]
            V = cv.bf([128, NT, 128]); rV = [Res("Vs%d" % j) for j in range(NSB)]
            qk = cv.f32([128, 8, 64]); rqk = Res("qk")
            sq = cv.f32([128, 8, 64]); rsq = Res("sqs")
            stg = cv.bf([128, 4, 128]); rstg = Res("stg")
            Pm = Rot([(cv.bf([128, 2, 384]), Res("Pm%d" % i)) for i in range(2)])
            rec = cv.f32([128, 384]); rrec = Res("recs")
            load(W, wb_in[l, :, 1184:1824].rearrange("(c p) n -> p c n", p=128), [rW], [r_wb[("in", l)]])
            load(G_sw[:].rearrange("p a b -> p (a b)"), swg_d[l, :].partition_broadcast(128), [rG])
            load(ESINK[:], sink_d[l], [rG])
            S.op("act", lambda e: e.activation(ESINK[:], ESINK[:], AF.Exp), reads=[rG], writes=[rG])
            Oacc, rO = PS[5]
            Dacc, rD = PS[7]

            def body(j):
                for t in range(4):
                    tg = 4 * j + t
                    p1, rp1 = rot4.next()
                    for kc in range(8):
                        S.op("pe", lambda e, p1=p1, kc=kc, t=t: e.matmul(
                            p1[:], lhsT=hT[:, kc, t * 128:(t + 1) * 128], rhs=W[:, kc, 0:512],
                            start=(kc == 0), stop=(kc == 7)), reads=[rW, rhT], writes=[rp1])
                    p2, rp2 = rot4.next()
                    for kc in range(8):
                        S.op("pe", lambda e, p2=p2, kc=kc, t=t: e.matmul(
                            p2[:, 0:128], lhsT=hT[:, kc, t * 128:(t + 1) * 128], rhs=W[:, kc, 512:640],
                            start=(kc == 0), stop=(kc == 7)), reads=[rW, rhT], writes=[rp2])
                    S.op("act", lambda e, p1=p1: e.copy(qk, p1[:].rearrange("p (a b) -> p a b", a=8)), reads=[rp1], writes=[rqk])
                    S.op("act", lambda e, p2=p2, tg=tg: e.copy(V[:, tg, :], p2[:, 0:128]), reads=[rp2], writes=[rV[j]])
                    S.op("dve", lambda e: e.tensor_tensor(sq, qk, qk, ALU.mult), reads=[rqk], writes=[rsq])
                    S.op("dve", lambda e: e.tensor_reduce(stat[:, 16:24], sq, axis=AX.X, op=ALU.add), reads=[rsq], writes=[rstat16])
                    rstd_from_ss(16, 8, 64, rstat16)
                    S.op("dve", lambda e: e.tensor_tensor(qk, qk, stat[:, 16:24].unsqueeze(2).to_broadcast([128, 8, 64]), ALU.mult),
                         reads=[rqk, rstat16], writes=[rqk])
                    S.op("dve", lambda e: e.tensor_tensor(
                        stg[:, 0:3, :].rearrange("p r (g d) -> p r g d", g=2),
                        qk[:, 0:6, :].rearrange("p (g r) d -> p r g d", g=2),
                        G_sw[:, 0, :].unsqueeze(1).unsqueeze(1).to_broadcast([128, 3, 2, 64]), ALU.mult),
                        reads=[rqk, rG], writes=[rstg])
                    S.op("dve", lambda e: e.tensor_tensor(stg[:, 3, :].rearrange("p (g d) -> p g d", g=2), qk[:, 6:8, :], G_sw[:, 1, :].unsqueeze(1).to_broadcast([128, 2, 64]), ALU.mult),
                         reads=[rqk, rG], writes=[rstg])
                    pt, rp = rot4.next()
                    for c in range(4):
                        transpose_to(pt[:, c * 128:(c + 1) * 128], stg[:, c, :], rstg, rp)
                    S.op("act", lambda e, pt=pt, t=t: e.copy(QT[:, :, t * 128:(t + 1) * 128], pt[:, 0:384].rearrange("p (a b) -> p a b", a=3)),
                         reads=[rp], writes=[rQT])
                    S.op("act", lambda e, pt=pt, tg=tg: e.copy(KT[:, tg * 128:(tg + 1) * 128], pt[:, 384:512]), reads=[rp], writes=[rKT[j]])
                yield "proj"
                for qi in range(4):
                    n = 4 * j + qi
                    blks = [1] if n == 0 else [0, 1]
                    for g in range(2):
                        b0 = 64 * g
                        Zs = {}
                        for blk in blks:
                            kbi = n - 1 + blk
                            Z, rZ = rot4.next()
                            Zs[blk] = (Z, rZ)
                            S.op("pe", lambda e, Z=Z, b0=b0, kbi=kbi, qi=qi: e.matmul(
                                Z[:, 0:384].rearrange("p (a b) -> p a b", a=3), lhsT=KT[b0:b0 + 64, kbi * 128:(kbi + 1) * 128],
                                rhs=QT[b0:b0 + 64, :, qi * 128:(qi + 1) * 128], start=True, stop=True),
                                reads=[rKT[kbi // 4], rQT], writes=[rZ])
                        PM, rPM = Pm.next()
                        for blk in blks:
                            Z, rZ = Zs[blk]
                            S.op("act", lambda e, Z=Z, blk=blk, PM=PM: e.activation(PM[:, blk, :], Z[:, 0:384], AF.Exp, scale=0.125),
                                 reads=[rZ], writes=[rPM])
                        for blk in blks:
                            S.op("dve", lambda e, PM=PM, blk=blk, g=g: e.tensor_tensor(
                                PM[:, blk, :].rearrange("p (r q) -> p r q", r=3), PM[:, blk, :].rearrange("p (r q) -> p r q", r=3),
                                ESW[:, 3 * g:3 * g + 3, blk, :], ALU.mult), reads=[rPM, rESW], writes=[rPM])
                        for r in range(3):
                            h = 3 * g + r
                            pr = h // 2
                            ob = 64 * (h % 2)
                            for bi, blk in enumerate(blks):
                                kbi = n - 1 + blk
                                S.op("pe", lambda e, PM=PM, ob=ob, pr=pr, kbi=kbi, g=g, blk=blk, r=r, bi=bi, nb=len(blks): e.matmul(
                                    Oacc[ob:ob + 64, pr * 128:(pr + 1) * 128], lhsT=V[:, kbi, g * 64:(g + 1) * 64],
                                    rhs=PM[:, blk, r * 128:(r + 1) * 128], start=(bi == 0), stop=(bi == nb - 1)),
                                    reads=[rPM, rV[kbi // 4]], writes=[rO])
                                S.op("pe", lambda e, PM=PM, ob=ob, pr=pr, blk=blk, r=r, bi=bi, nb=len(blks): e.matmul(
                                    Dacc[ob:ob + 64, pr * 128:(pr + 1) * 128], lhsT=ones[:, 0:64],
                                    rhs=PM[:, blk, r * 128:(r + 1) * 128], start=(bi == 0), stop=(bi == nb - 1)),
                                    reads=[rPM, rCB], writes=[rD])
                        yield "it"
                    S.op("dve", lambda e: e.tensor_tensor(rec.rearrange("p (a b) -> p a b", a=3), Dacc[:, 0:384].rearrange("p (a b) -> p a b", a=3),
                                                          ESINK[:].unsqueeze(2).to_broadcast([128, 3, 128]), ALU.add),
                         reads=[rD, rG], writes=[rrec])
                    S.op("dve", lambda e: e.reciprocal(rec, rec), reads=[rrec], writes=[rrec])
                    S.op("dve", lambda e, n=n: e.tensor_tensor(mixT[:, 5:8, n * 128:(n + 1) * 128], Oacc[:, 0:384].rearrange("p (a b) -> p a b", a=3),
                                                               rec.rearrange("p (a b) -> p a b", a=3), ALU.mult),
                         reads=[rO, rrec], writes=[rmix[j]])
            return body

        def out_proj(l, s):
            cv = Carve(ATT_BASE)
            WO = cv.bf([128, 8, D]); rWO = Res("WO")
            M2 = cv.f32([128, D]); rM2 = Res("M2")
            tmpb = Rot([(cv.f32([128, 512]), Res("tmpo%d" % i)) for i in range(2)])
            load(WO, wb_out[l].rearrange("(c p) n -> p c n", p=128), [rWO], [r_wb[("out", l)]])
            load_mods(l, s, 2, M2, rM2)
            if dbg and "mixT" in dbg and l == 0 and s == 0:
                for c in range(8):
                    tmp, rtmp = tmpb.next()
                    for q0 in range(0, SEQ, 512):
                        S.op("dve", lambda e, tmp=tmp, c=c, q0=q0: e.tensor_copy(tmp, mixT[:, c, q0:q0 + 512]), reads=rmix, writes=[rtmp])
                        S.dma(sp_q, lambda e, tmp=tmp, c=c, q0=q0: e.dma_start(out=dbg_outs["mixT"][c * 128:(c + 1) * 128, q0:q0 + 512], in_=tmp),
                              reads=[rtmp])
            for tg in range(NT):
                for half in range(2):
                    pt, rp = rot4.next()
                    for c in range(8):
                        S.op("pe", lambda e, pt=pt, c=c, tg=tg, half=half: e.matmul(
                            pt[:], lhsT=mixT[:, c, tg * 128:(tg + 1) * 128], rhs=WO[:, c, half * 512:(half + 1) * 512],
                            start=(c == 0), stop=(c == 7)), reads=[rmix[tg // 4], rWO], writes=[rp])
                    tmp, rtmp = tmpb.next()
                    residual_update(pt[:], rp, tg, half, M2, rM2, tmp, rtmp)
            S.barrier()

        def ffn(l, s, last):
            cv = Carve(0)
            aT = cv.bf([128, NFC, 512]); raT = Res("aT")
            WU = Rot([(cv.bf([128, 8, 256]), Res("WU%d" % i)) for i in range(4)])
            WD = Rot([(cv.bf([128, D]), Res("WD%d" % i)) for i in range(6)])
            UB = [Rot([(cv.f32([128, 514]), Res("UB%d_%d" % (gv, i))) for i in range(3)]) for gv in range(2)]
            ACC = [Rot([(cv.f32([128, 512]), Res("ACC%d_%d" % (gv, i))) for i in range(3)]) for gv in range(2)]
            SG = Rot([(cv.f32([128, 512]), Res("SG%d" % i)) for i in range(3)])
            hT2 = cv.bf([128, 8, 512]); rhT2 = Res("hT2")
            xn2 = cv.f32([128, D]); rxn2 = Res("xn2")
            hbf2 = cv.bf([128, D]); rhbf2 = Res("hbf2")
            NB = Rot([(hT, rhT, xn[:], rxn, hbf[:], rhbf, 0, rstat), (hT2, rhT2, xn2, rxn2, hbf2, rhbf2, 1, rstat1)])
            halo = cv.f32([128, 44, 2]); rhalo = [Res("halo%d" % c) for c in range(44)]
            M2 = cv.f32([128, D]); rM2 = Res("M2f")
            tmpb = Rot([(cv.f32([128, 512]), Res("tmpf%d" % i)) for i in range(2)])
            load(CW[:], cw_d[l], [rG])
            load(CBI[:], cbias_d[l], [rG])
            load_mods(l, s, 5, M2, rM2)
            S.op("dve", lambda e: e.memset(halo, 0.0), writes=rhalo)
            rs_all = cv.f32([128, NT]); rrs = Res("rs_all")
            for tg in range(NT):
                S.op("act", lambda e, tg=tg: e.activation(hbf2, X[:, tg, :], AF.Square, accum_out=rs_all[:, tg:tg + 1]),
                     reads=[rX[tg]], writes=[rhbf2, rrs])
            S.op("act", lambda e: e.activation(rs_all, rs_all, AF.Ln, bias=EPS, scale=1.0 / D), reads=[rrs], writes=[rrs])
            S.op("act", lambda e: e.activation(rs_all, rs_all, AF.Exp, scale=-0.5), reads=[rrs], writes=[rrs])
            pre = (rs_all, rrs)
            hnext = norm_and_transpose(0, NB.next(), pre)
            for j in range(NSB):
                hTc, rhTc = hnext
                for i in range(NFC):
                    wu, rwu = WU.next()
                    load(wu, wb_up[l, i], [rwu], [r_wb[("up", l)]])
                    accs = []
                    for gv in range(2):
                        ch = gv * NFC + i
                        pt, rp = rot4.next()
                        for kc in range(8):
                            S.op("pe", lambda e, pt=pt, kc=kc, wu=wu, gv=gv, hTc=hTc: e.matmul(
                                pt[:], lhsT=wu[:, kc, gv * 128:(gv + 1) * 128], rhs=hTc[:, kc, :], start=(kc == 0), stop=(kc == 7)),
                                reads=[rwu, rhTc], writes=[rp])
                        ub, rub = UB[gv].next()
                        acc, racc = ACC[gv].next()
                        S.op("act", lambda e, pt=pt, ub=ub: e.copy(ub[:, 2:514], pt[:]), reads=[rp], writes=[rub])
                        S.op("act", lambda e, pt=pt, acc=acc, ch=ch: e.activation(acc, pt[:], AF.Identity, bias=CBI[:, ch:ch + 1],
                                                                                 scale=CW[:, ch, 2:3]),
                             reads=[rp, rG], writes=[racc])
                        S.op("dve", lambda e, ub=ub, ch=ch: e.tensor_copy(ub[:, 0:2], halo[:, ch, :]), reads=[rhalo[ch]], writes=[rub])
                        S.op("dve", lambda e, ub=ub, ch=ch: e.tensor_copy(halo[:, ch, :], ub[:, 512:514]), reads=[rub], writes=[rhalo[ch]])
                        S.op("dve", lambda e, ub=ub, acc=acc, ch=ch: e.scalar_tensor_tensor(acc, ub[:, 1:513], CW[:, ch, 1:2], acc, ALU.mult, ALU.add),
                             reads=[rub, racc, rG], writes=[racc])
                        S.op("dve", lambda e, ub=ub, acc=acc, ch=ch: e.scalar_tensor_tensor(acc, ub[:, 0:512], CW[:, ch, 0:1], acc, ALU.mult, ALU.add),
                             reads=[rub, racc, rG], writes=[racc])
                        accs.append((acc, racc))
                    sg, rsg = SG.next()
                    S.op("act", lambda e, sg=sg, acc=accs[0][0]: e.activation(sg, acc, AF.Silu), reads=[accs[0][1]], writes=[rsg])
                    S.op("dve", lambda e, sg=sg, i=i, acc=accs[1][0]: e.tensor_tensor(aT[:, i, :], acc, sg, ALU.mult),
                         reads=[accs[1][1], rsg], writes=[raT])
                if j + 1 < NSB:
                    hnext = norm_and_transpose(j + 1, NB.next(), pre)
                for i in range(NFC):
                    wd, rwd = WD.next()
                    load(wd, wb_down[l, i * 128:(i + 1) * 128, :], [rwd], [r_wb[("down", l)]])
                    for t in range(4):
                        for half in range(2):
                            pt, rp = PS[2 * t + half]
                            S.op("pe", lambda e, pt=pt, wd=wd, i=i, t=t, half=half: e.matmul(
                                pt[:], lhsT=aT[:, i, t * 128:(t + 1) * 128], rhs=wd[:, half * 512:(half + 1) * 512],
                                start=(i == 0), stop=(i == NFC - 1)), reads=[raT, rwd], writes=[rp])
                for t in range(4):
                    tg = 4 * j + t
                    for half in range(2):
                        pt, rp = PS[2 * t + half]
                        tmp, rtmp = tmpb.next()
                        residual_update(pt[:], rp, tg, half, M2, rM2, tmp, rtmp)
                    if last:
                        S.dma(sp_q, lambda e, tg=tg: e.dma_start(out=out_d[s, tg * 128:(tg + 1) * 128, :], in_=X[:, tg, :]),
                              reads=[rX[tg]], free=True)
            S.barrier()

        def setup_esw():
            cv = Carve(ATT_BASE)
            ESWf = cv.f32([128, 6, 2, 128]); rESWf = Res("ESWf")
            load(ESWf.rearrange("p a b c -> p a (b c)"), relg_d, [rESWf])
            S.op("act", lambda e: e.activation(ESWf, ESWf, AF.Exp), reads=[rESWf], writes=[rESWf])
            S.op("dve", lambda e: e.tensor_tensor(ESW, ESWf, CF[:, 64:320].rearrange("p (b c) -> p b c", b=2).unsqueeze(1).to_broadcast([128, 6, 2, 128]),
                                                  ALU.mult), reads=[rESWf, rCF], writes=[rESW])
            S.barrier()

        setup_esw()

        def dump_x(s):
            for tg in range(NT):
                S.dma(sp_q, lambda e, tg=tg: e.dma_start(out=out_d[s, tg * 128:(tg + 1) * 128, :], in_=X[:, tg, :]), reads=[rX[tg]])

        for s in range(NSEQ):
            for tg in range(NT):
                load(X[:, tg, :], x_d[s, tg * 128:(tg + 1) * 128, :], [rX[tg]], free=True)
            if stop_after == "pro":
                dump_x(s); continue
            rope_tables(s)
            if stop_after == "rope":
                dump_x(s); continue
            for l in range(DEPTH):
                if s == 0:
                    compute_mods(l)
                    if l == 0:
                        convert_rest()
                setup_norm_mods(l, s, 0)
                rot4.items = PS[0:4] + ([PS[6]] if WARM_FILL[0] <= 0 else [])
                rot4.i = 0
                S.warm_epochs.add(S.epoch)
                cvac = Carve(ATT_BASE)
                bodyA = pass_A(l, s, cvac)
                bodyC = pass_C(l, s, cvac)
                norm_and_transpose(0)
                for j in range(NSB):
                    gA, gC = bodyA(j), bodyC(j)
                    next(gA)
                    next(gC)
                    if j + 1 < NSB:
                        norm_and_transpose(j + 1)
                    nA, nC = 4 * (4 * j + 4), 8
                    cdone = 0
                    for i in range(nA):
                        next(gA)
                        while cdone < nC and cdone * nA < (i + 1) * nC:
                            next(gC)
                            cdone += 1
                    for g_ in (gA, gC):
                        for _ in g_:
                            pass
                S.barrier()
                rot4.items = PS[0:4] + ([PS[7]] if WARM_FILL[0] > 0 else PS[6:8])
                rot4.i = 0
                if stop_after == "A":
                    dump_x(s); break
                S.warm_epochs.add(S.epoch)
                pass_B(l, s)
                rot4.items = PS[0:4] + PS[6:8]
                rot4.i = 0
                if stop_after == "B":
                    dump_x(s); break
                out_proj(l, s)
                if stop_after == "C":
                    dump_x(s); break
                setup_norm_mods(l, s, 1)
                ffn(l, s, last=(l == DEPTH - 1))
        if SCHEDULE[0]:
            S.schedule()
        else:
            S._resolve_gates()
        if EXPERIMENT[0]:
            return nc, S
        S.emit(sems, dsems)
    return nc, S


def make_in_maps(inputs, n_cores, nseq, seq, depth):
    f = np.float32
    cbf, cf, bidx = _host_consts()
    rel_table = np.asarray(inputs["rel_table"], f)
    relg = np.ascontiguousarray(np.transpose(rel_table[bidx], (0, 3, 1, 2))).reshape(128, 6, 256)
    nt = seq // 128
    sw_g = np.concatenate([np.asarray(inputs["sw_qn_g"], f), np.asarray(inputs["sw_kn_g"], f)], axis=1).reshape(depth, 128)
    sinks = np.asarray(inputs["sw_sinks"], f)
    sinkT = np.zeros((depth, 128, 3), f)
    for pr in range(3):
        sinkT[:, 0:64, pr] = sinks[:, 2 * pr][:, None]
        sinkT[:, 64:128, pr] = sinks[:, 2 * pr + 1][:, None]
    conv_w = np.asarray(inputs["conv_w"], f)
    cwT = np.ascontiguousarray(np.transpose(conv_w.reshape(depth, 3, 44, 128), (0, 3, 2, 1)))
    cbT = np.ascontiguousarray(np.transpose(np.asarray(inputs["conv_b"], f).reshape(depth, 44, 128), (0, 2, 1)))
    shared = {
        "relg": relg, "norm1_g": np.asarray(inputs["norm1_g"], f), "norm2_g": np.asarray(inputs["norm2_g"], f),
        "w_ada": np.asarray(inputs["w_ada"], f), "b_ada": np.asarray(inputs["b_ada"], f),
        "w_in": np.asarray(inputs["w_in"], f), "mla_cq_g": np.asarray(inputs["mla_cq_g"], f),
        "w_uq": np.asarray(inputs["w_uq"], f), "mla_ckv_g": np.asarray(inputs["mla_ckv_g"], f),
        "w_ukv": np.asarray(inputs["w_ukv"], f), "mla_qn_g": np.asarray(inputs["mla_qn_g"], f),
        "mla_kn_g": np.asarray(inputs["mla_kn_g"], f), "sw_g": sw_g, "sinkT": sinkT,
        "w_out": np.asarray(inputs["w_out"], f), "w_up": np.asarray(inputs["w_up"], f),
        "cwT": cwT, "cbT": cbT, "w_down": np.asarray(inputs["w_down"], f), "cbf": cbf, "cf": cf,
    }
    x = np.asarray(inputs["x"], f)
    c = np.asarray(inputs["c"], f)
    pos = np.asarray(inputs["positions"], np.int32)
    maps = []
    for core in range(n_cores):
        b0 = core * nseq
        m = dict(shared)
        m["x"] = np.ascontiguousarray(x[b0:b0 + nseq])
        m["cT"] = np.ascontiguousarray(np.transpose(c[b0:b0 + nseq].reshape(nseq, 8, 128), (2, 1, 0)))
        m["posT"] = np.ascontiguousarray(np.transpose(pos[b0:b0 + nseq].reshape(nseq, nt, 128), (0, 2, 1)))
        maps.append(m)
    return maps


_NC_CACHE = {}


def kernel(**inputs):
    x = np.asarray(inputs["x"])
    B, SEQ, _ = x.shape
    depth = np.asarray(inputs["w_in"]).shape[0]
    nseq = B // N_CORES
    key = (SEQ, depth, nseq)
    if key not in _NC_CACHE:
        _NC_CACHE[key] = build_nc(SEQ=SEQ, DEPTH=depth, NSEQ=nseq)[0]
    nc = _NC_CACHE[key]
    maps = make_in_maps(inputs, N_CORES, nseq, SEQ, depth)
    res = run_bass_kernel_spmd(nc, maps, core_ids=list(range(N_CORES)))
    out = np.concatenate([np.asarray(r["out"]) for r in res.results], axis=0)
    return out.astype(np.float32)
```

```python
import contextlib
import math
import numpy as np
import ml_dtypes
import concourse.bass as bass
import concourse.mybir as mybir
from concourse.bass_utils import run_bass_kernel_spmd

F32 = mybir.dt.float32
BF16 = mybir.dt.bfloat16
I32 = mybir.dt.int32
AF = mybir.ActivationFunctionType
ALU = mybir.AluOpType
AX = mybir.AxisListType

N_DMA_SEMS = 32
N_HW_SEMS = 24
D = 1024
D_IN = 1824
D_FF = 2816
NFC = 22
EPS = 1e-6
N_CORES = 8


class Res:
    __slots__ = ("name", "lw", "rd", "excl")

    def __init__(self, name, excl=False):
        self.name = name
        self.lw = None
        self.rd = []
        self.excl = excl


class Op:
    __slots__ = ("eng", "idx", "fn", "deps", "odeps", "signaled", "dma", "dsem", "dval", "prewait",
                 "cost", "lat", "pidx", "npred", "succ", "ready", "fin", "epoch", "gate_of", "free", "tail")

    def __init__(self, eng, fn, dma=False):
        self.eng = eng
        self.fn = fn
        self.deps = []
        self.odeps = []
        self.signaled = False
        self.dma = dma
        self.dsem = None
        self.dval = None
        self.prewait = None
        self.cost = 100.0
        self.lat = 0.0
        self.epoch = 0
        self.gate_of = None
        self.free = False


class _Rec:
    def __init__(self):
        self.calls = []

    def __getattr__(self, name):
        def f(*a, **k):
            self.calls.append((name, a, k))
            return self
        return f


def _free(ap):
    n = 1
    for d in ap.shape[1:]:
        n *= int(d)
    return n


def _is_psum(ap):
    return "psum" in str(ap.space).lower() or "PSUM" in str(ap.space)


def _estimate(o):
    rec = _Rec()
    try:
        o.fn(rec)
    except Exception:
        return
    if not rec.calls:
        return
    name, a, k = rec.calls[0]
    aps = [x for x in list(a) + list(k.values()) if hasattr(x, "shape") and hasattr(x, "dtype")]
    if not aps:
        return
    out = k.get("out", a[0] if a else aps[0])
    if o.dma:
        nbytes = _free(out) * int(out.shape[0]) * mybir.dt.size(out.dtype)
        o.cost = 60.0
        o.lat = 2000.0 + nbytes / 160.0
        return
    if o.eng == "pe":
        rhs = k.get("rhs", a[2] if len(a) > 2 else out)
        o.cost = max(_free(rhs), 64) / 2.0 + 16.0
    elif o.eng == "act":
        o.cost = (_free(out) + 224.0) / 1.2
    else:
        n = _free(out)
        psum = any(_is_psum(x) for x in aps)
        small = all(mybir.dt.size(x.dtype) == 2 for x in aps)
        speed = 1.0
        if not psum:
            if name in ("tensor_copy", "tensor_scalar", "memset"):
                speed = 4.0 if small else 2.0
            elif small:
                speed = 2.0
        base = 120.0 if psum else 70.0
        o.cost = (base + n / speed) / 0.96
        if o.eng == "pool":
            o.cost *= 2.0


class Sched:
    ENGS = ("pe", "act", "dve", "pool", "sp")

    def __init__(self, nc):
        self.nc = nc
        self.prog = {e: [] for e in self.ENGS}
        self.dma_rr = 0
        self.dma_rr_sw = N_HW_SEMS
        self.dma_cnt = [0] * N_DMA_SEMS
        self.dma_last = [None] * N_DMA_SEMS
        self.need_gate = {e: None for e in self.ENGS}
        self.epoch = 0
        self.epoch_ops = [[]]
        self.nops = 0
        self.gate = {e: None for e in self.ENGS}
        self.warm_fn = None
        self.warm_dep = None
        self.warm_epochs = set()
        self.n_dummies = 0

    def _add(self, o):
        o.epoch = self.epoch
        if o.free:
            o.idx = len(self.prog[o.eng])
            o.pidx = self.nops
            self.nops += 1
            self.prog[o.eng].append(o)
            _estimate(o)
            return o
        if self.need_gate[o.eng] is not None:
            o.gate_of = self.need_gate[o.eng]
            self.need_gate[o.eng] = None
            self.gate[o.eng] = o
        elif self.gate[o.eng] is not None:
            o.odeps.append(self.gate[o.eng])
        o.idx = len(self.prog[o.eng])
        o.pidx = self.nops
        self.nops += 1
        self.prog[o.eng].append(o)
        self.epoch_ops[-1].append(o)
        _estimate(o)
        return o

    def op(self, eng, fn, reads=(), writes=()):
        o = Op(eng, fn)
        self._deps(o, reads, writes)
        return self._add(o)

    def dma(self, eng, fn, reads=(), writes=(), free=False):
        o = Op(eng, fn, dma=True)
        o.free = free
        self._deps(o, reads, writes)
        if eng == "pool":
            i = self.dma_rr_sw
            self.dma_rr_sw = N_HW_SEMS + (self.dma_rr_sw + 1 - N_HW_SEMS) % (N_DMA_SEMS - N_HW_SEMS)
        else:
            i = self.dma_rr
            self.dma_rr = (self.dma_rr + 1) % N_HW_SEMS
        o.prewait = self.dma_last[i]
        self.dma_cnt[i] += 16
        o.dsem = i
        o.dval = self.dma_cnt[i]
        self.dma_last[i] = o
        return self._add(o)

    def _dep(self, o, d):
        if d is o:
            return
        if d.eng == "pe" and o.eng == "pe" and not d.dma:
            for x in o.odeps:
                if x is d:
                    return
            o.odeps.append(d)
            return
        for x in o.deps:
            if x is d:
                return
        o.deps.append(d)
        if not d.dma:
            d.signaled = True

    def _deps(self, o, reads, writes):
        for r in reads:
            if r.lw is not None:
                self._dep(o, r.lw)
            if r.excl:
                for d in r.rd:
                    if d.eng != o.eng:
                        self._dep(o, d)
        for w in writes:
            if w.lw is not None:
                self._dep(o, w.lw)
            for d in w.rd:
                self._dep(o, d)
        for r in reads:
            r.rd.append(o)
        for w in writes:
            w.lw = o
            w.rd = []

    def barrier(self):
        for e in self.ENGS:
            self.need_gate[e] = self.epoch
        self.epoch += 1
        self.epoch_ops.append([])

    def schedule(self):
        SEM_LAT = 120.0
        allops = []
        for e in self.ENGS:
            allops.extend(self.prog[e])
        for o in allops:
            o.succ = []
            o.npred = 0
            o.ready = 0.0
            o.fin = None
        for o in allops:
            preds = list(o.deps) + list(o.odeps)
            if o.prewait is not None:
                preds.append(o.prewait)
            if o.gate_of is not None:
                preds.extend(self.epoch_ops[o.gate_of])
            o.npred = len(preds)
            for d in preds:
                d.succ.append(o)
        byp = sorted(allops, key=lambda o: o.pidx)
        for o in byp:
            o.tail = 0.0
        for o in reversed(byp):
            t = 0.0
            for sc in o.succ:
                if sc.tail > t:
                    t = sc.tail
            o.tail = t + o.cost + (o.lat if o.dma else 0.0) + SEM_LAT
        ready = {e: [] for e in self.ENGS}
        for o in allops:
            if o.npred == 0:
                ready[o.eng].append(o)
        tfree = {e: 0.0 for e in self.ENGS}
        xfer_free = [0.0]
        newprog = {e: [] for e in self.ENGS}
        remaining = len(allops)
        while remaining:
            best = None
            best_start = None
            for e in self.ENGS:
                lst = ready[e]
                if not lst:
                    continue
                tf = tfree[e]
                cand = None
                for o in lst:
                    if o.ready <= tf:
                        if cand is None or cand.ready > tf or (PRIO_CP[0] and o.tail > cand.tail) or (not PRIO_CP[0] and o.pidx < cand.pidx):
                            cand = o
                    elif cand is None or (cand.ready > tf and (o.ready, o.pidx) < (cand.ready, cand.pidx)):
                        cand = o
                st = max(tf, cand.ready)
                if best is None or st < best_start or (st == best_start and cand.pidx < best.pidx):
                    best, best_start = cand, st
            o = best
            e = o.eng
            ready[e].remove(o)
            start = best_start
            tfree[e] = start + o.cost
            if o.dma:
                xs = max(start + o.cost, xfer_free[0])
                xfer_free[0] = xs + max(o.lat - 2000.0, 0.0)
                o.fin = xfer_free[0] + 2000.0
            else:
                o.fin = start + o.cost
            newprog[e].append(o)
            remaining -= 1
            for sc in o.succ:
                lat = 0.0 if (sc.eng == o.eng and not o.dma and o.eng == "pe") else SEM_LAT
                t = o.fin + lat
                if t > sc.ready:
                    sc.ready = t
                sc.npred -= 1
                if sc.npred == 0:
                    ready[sc.eng].append(sc)
        for e in self.ENGS:
            assert len(newprog[e]) == len(self.prog[e])
        if self.warm_fn is not None and self.warm_epochs:
            filled = []
            prev_fin = None
            prev_epoch = None
            ndum = 0
            for o in newprog["pe"]:
                st = o.fin - o.cost
                if prev_fin is not None and o.epoch in self.warm_epochs and not o.free and prev_epoch == o.epoch:
                    gap = st - prev_fin
                    n = int(WARM_FILL[0] * gap / WARM_COST)
                    for _ in range(min(n, 64)):
                        d = Op("pe", self.warm_fn)
                        d.free = True
                        d.epoch = o.epoch
                        d.cost = WARM_COST
                        d.fin = 0.0
                        d.pidx = -1
                        if self.warm_dep is not None:
                            d.deps.append(self.warm_dep)
                        filled.append(d)
                        ndum += 1
                filled.append(o)
                prev_fin = o.fin
                prev_epoch = o.epoch
            newprog["pe"] = filled
            self.n_dummies = ndum
        for e in self.ENGS:
            self.prog[e] = newprog[e]
            for i, o in enumerate(newprog[e]):
                o.idx = i
        self.est_ns = max(tfree.values())
        self._resolve_gates()

    def _resolve_gates(self):
        for e in self.ENGS:
            for g in self.prog[e]:
                if g.gate_of is None:
                    continue
                last = {}
                for o in self.epoch_ops[g.gate_of]:
                    if o.dma:
                        g.deps.append(o)
                    elif o.eng != g.eng:
                        if o.eng not in last or o.idx > last[o.eng].idx:
                            last[o.eng] = o
                for o in last.values():
                    g.deps.append(o)
                    o.signaled = True

    def emit(self, sems, dsems, final_wait_eng="sp"):
        nc = self.nc
        for e in self.ENGS:
            for o in self.prog[e]:
                o.signaled = False
        for e in self.ENGS:
            for o in self.prog[e]:
                last = {}
                for d in o.deps:
                    if d.dma:
                        continue
                    if d.eng not in last or d.idx > last[d.eng].idx:
                        last[d.eng] = d
                for d in last.values():
                    d.signaled = True
        cnt = {}
        for e in self.ENGS:
            c = 0
            arr = []
            for o in self.prog[e]:
                if o.signaled and not o.dma:
                    c += 1
                arr.append(c)
            cnt[e] = arr
        engobj = {"pe": "tensor", "act": "scalar", "dve": "vector", "pool": "gpsimd", "sp": "sync"}
        self.n_waits = 0
        with nc.Block() as block:
            for e in self.ENGS:
                ops = self.prog[e]
                if not ops and e != final_wait_eng:
                    continue

                def body(eng, e=e, ops=ops):
                    waited = {}

                    def wait_for(d):
                        if d.dma:
                            key = ("d", d.dsem)
                            val = d.dval
                            sem = dsems[d.dsem]
                        else:
                            key = ("e", d.eng)
                            val = cnt[d.eng][d.idx]
                            sem = sems[d.eng]
                        if waited.get(key, 0) >= val:
                            return
                        waited[key] = val
                        eng.wait_ge(sem, val)
                        self.n_waits += 1

                    for o in ops:
                        dl = list(o.deps)
                        if o.dma and o.prewait is not None:
                            dl.append(o.prewait)
                        best = {}
                        for d in dl:
                            if d.dma:
                                key, val = ("d", d.dsem), d.dval
                            else:
                                key, val = ("e", d.eng), cnt[d.eng][d.idx]
                            if key not in best or val > best[key][0]:
                                best[key] = (val, d)
                        for key in best:
                            wait_for(best[key][1])
                        ins = o.fn(eng)
                        if o.dma:
                            ins.then_inc(dsems[o.dsem], 16)
                        elif o.signaled:
                            ins.then_inc(sems[e], 1)
                    if e == final_wait_eng:
                        for i in range(N_DMA_SEMS):
                            if self.dma_last[i] is not None:
                                wait_for(self.dma_last[i])

                getattr(block, engobj[e])(body)


class Rot:
    def __init__(self, items):
        self.items = items
        self.i = 0

    def next(self):
        it = self.items[self.i]
        self.i = (self.i + 1) % len(self.items)
        return it


def _t5_bucket(dist):
    max_exact = 16
    n = np.maximum(dist, 0)
    nf = np.maximum(n, 1).astype(np.float32)
    large = max_exact + (np.log(nf / np.float32(max_exact)) / np.float32(math.log(128 / max_exact))
                         * np.float32(32 - max_exact)).astype(np.int32)
    large = np.minimum(large, 31)
    return np.where(n < max_exact, n, large)


def _host_consts():
    bf = ml_dtypes.bfloat16
    k = np.arange(128)[:, None]
    q = np.arange(128)[None, :]
    cb = np.zeros((128, 6, 128), np.float32)
    cb[:, 0] = np.eye(128)
    cb[:, 1] = -1.0 * (k >= q)
    cb[:, 2] = -1.0
    cb[:, 3] = 1.0
    cb[:, 4] = (k < q)
    cb[:, 5] = (k <= q)
    cbf = cb.astype(bf)
    invf = np.power(np.float32(10000.0), -np.arange(16, dtype=np.float32) / np.float32(16)).astype(np.float32)
    cf = np.zeros((128, 64 + 256), np.float32)
    cf[:, 0:16] = invf
    cf[:, 16:32] = invf
    cf[:, 32:48] = 0.0
    cf[:, 48:64] = np.float32(math.pi / 2)
    m = np.zeros((128, 2, 128), np.float32)
    m[:, 0] = (k > q)
    m[:, 1] = (k <= q)
    cf[:, 64:320] = m.reshape(128, 256)
    bidx = np.zeros((128, 2, 128), np.int64)
    bidx[:, 0] = _t5_bucket(128 + q - k)
    bidx[:, 1] = _t5_bucket(q - k)
    return cbf, cf, bidx


DBG_CUT = [0]
SCHEDULE = [True]
PRIO_CP = [False]
WARM_FILL = [0.6]
WARM_COST = 144.0
WARM_BANK = [6]
WARM_B = [False]
DEPTHS = {"E": 3, "SP": 2, "TM": 2, "W": 2}
ARENA_KB = [109]
EXPERIMENT = [False]


def build_nc(SEQ=2048, DEPTH=2, NSEQ=2, dbg=None, stop_after=None):
    NT = SEQ // 128
    NSB = SEQ // 512
    nc = bass.Bass("TRN2", target_bir_lowering=False)

    def din(name, shape, dt=F32):
        return nc.dram_tensor(name, list(shape), dt, kind="ExternalInput").ap()

    x_d = din("x", [NSEQ, SEQ, D])
    cT_d = din("cT", [128, 8, NSEQ])
    posT_d = din("posT", [NSEQ, 128, NT], I32)
    relg_d = din("relg", [128, 6, 256])
    norm1_d = din("norm1_g", [DEPTH, D])
    norm2_d = din("norm2_g", [DEPTH, D])
    w_ada_d = din("w_ada", [DEPTH, D, 6 * D])
    b_ada_d = din("b_ada", [DEPTH, 6 * D])
    w_in_d = din("w_in", [DEPTH, D, D_IN])
    cqg_d = din("mla_cq_g", [DEPTH, 256])
    w_uq_d = din("w_uq", [DEPTH, 256, 576])
    ckvg_d = din("mla_ckv_g", [DEPTH, 128])
    w_ukv_d = din("w_ukv", [DEPTH, 128, 768])
    qng_d = din("mla_qn_g", [DEPTH, 96])
    kng_d = din("mla_kn_g", [DEPTH, 96])
    swg_d = din("sw_g", [DEPTH, 128])
    sink_d = din("sinkT", [DEPTH, 128, 3])
    w_out_d = din("w_out", [DEPTH, D, D])
    w_up_d = din("w_up", [DEPTH, D, 2 * D_FF])
    cw_d = din("cwT", [DEPTH, 128, 44, 3])
    cbias_d = din("cbT", [DEPTH, 128, 44])
    w_down_d = din("w_down", [DEPTH, D_FF, D])
    cbf_d = din("cbf", [128, 6, 128], BF16)
    cf_d = din("cf", [128, 320])
    out_d = nc.dram_tensor("out", [NSEQ, SEQ, D], F32, kind="ExternalOutput").ap()

    def dscr(name, shape, dt):
        return nc.dram_tensor(name, list(shape), dt).ap()
    wb_ada = dscr("wb_ada", [DEPTH, D, 6 * D], BF16)
    wb_in = dscr("wb_in", [DEPTH, D, D_IN], BF16)
    wb_uq = dscr("wb_uq", [DEPTH, 256, 576], BF16)
    wb_ukv = dscr("wb_ukv", [DEPTH, 128, 768], BF16)
    wb_out = dscr("wb_out", [DEPTH, D, D], BF16)
    wb_up = dscr("wb_up_t", [DEPTH, NFC, 128, 8, 256], BF16)
    wb_down = dscr("wb_down", [DEPTH, D_FF, D], BF16)
    mods_d = dscr("mods", [DEPTH, NSEQ, 6 * D], F32)
    dbg_outs = {}
    if dbg:
        for name, shape in dbg.items():
            dbg_outs[name] = nc.dram_tensor("dbg_" + name, list(shape), F32, kind="ExternalOutput").ap()

    S = Sched(nc)
    with contextlib.ExitStack() as st:
        def sb(name, shape, dt):
            return st.enter_context(nc.sbuf_tensor(name, list(shape), dt))

        sems = {e: st.enter_context(nc.semaphore("s_" + e)) for e in Sched.ENGS}
        dsems = [st.enter_context(nc.semaphore("d%d" % i)) for i in range(N_DMA_SEMS)]

        PS = []
        for i in range(8):
            t = st.enter_context(nc.psum_tensor("ps%d" % i, [128, 512], F32))
            PS.append((t, Res("ps%d" % i, excl=True)))
        rot4 = Rot(PS[0:4] + PS[6:8])

        X = sb("X", [128, NT, D], F32 if not EXPERIMENT[0] else BF16)
        rX = [Res("X%d" % t) for t in range(NT)]
        CB = sb("CB", [128, 6, 128], BF16); rCB = Res("CB")
        CF = sb("CF", [128, 320], F32); rCF = Res("CF")
        ident = CB[:, 0, :]
        negU = CB[:, 1, :]
        negOnes = CB[:, 2, :]
        ones = CB[:, 3, :]
        maskS = CB[:, 4, :]
        maskC = CB[:, 5, :]
        M0 = sb("M0", [128, D], F32); rM0 = Res("M0")
        M1 = sb("M1", [128, D], F32); rM1 = Res("M1")
        G_cq = sb("G_cq", [128, 256], F32)
        G_ckv = sb("G_ckv", [128, 128], F32)
        G_q = sb("G_q", [128, 96], F32)
        G_k = sb("G_k", [128, 96], F32)
        G_sw = sb("G_sw", [128, 2, 64], F32)
        ESINK = sb("ESINK", [128, 3], F32)
        CW = sb("CW", [128, 44, 3], F32)
        CBI = sb("CBI", [128, 44], F32)
        rG = Res("gains")
        hT = sb("hT", [128, 8, 512], BF16); rhT = Res("hT")
        xn = sb("xn", [128, D], F32); rxn = Res("xn")
        hbf = sb("hbf", [128, D], BF16); rhbf = Res("hbf")
        SC = sb("sincos", [128, NT, 32], F32); rSC = Res("sincos")
        stat = sb("stat", [128, 64], F32); rstat = Res("stat"); rstat1 = Res("stat1"); rstat4 = Res("stat4"); rstat5 = Res("stat5"); rstat8 = Res("stat8"); rstat16 = Res("stat16")
        ESWt = sb("ESW", [128, 6, 2, 128], BF16); rESW = Res("ESW")
        ZT = sb("zeros", [128, 384], BF16)
        ESW = ESWt[:]
        ARENA_BYTES = ARENA_KB[0] * 1024
        ARENA = sb("arena", [128, ARENA_BYTES // 2], BF16)

        class Carve:
            def __init__(self, start=0):
                self.off = start

            def bf(self, shape):
                n = int(np.prod(shape[1:]))
                a = ARENA[:, self.off // 2: self.off // 2 + n]
                self.off += 2 * n
                assert self.off <= ARENA_BYTES, self.off
                return self._shape(a, shape)

            def f32(self, shape):
                n = int(np.prod(shape[1:]))
                self.off = (self.off + 3) // 4 * 4
                a = ARENA[:, self.off // 2: self.off // 2 + 2 * n].bitcast(F32)
                self.off += 4 * n
                assert self.off <= ARENA_BYTES, self.off
                return self._shape(a, shape)

            @staticmethod
            def _shape(a, shape):
                if shape[0] < 128:
                    a = a[0:shape[0]]
                if len(shape) == 2:
                    return a
                if len(shape) == 3:
                    return a.rearrange("p (a b) -> p a b", a=shape[1])
                if len(shape) == 4:
                    return a.rearrange("p (a b c) -> p a b c", a=shape[1], b=shape[2])
                raise ValueError

        cv0 = Carve()
        mixT = cv0.bf([128, 8, SEQ])
        rmix = [Res("mix%d" % j) for j in range(NSB)]
        ATT_BASE = cv0.off

        sp_q = "sp"

        def load(dst, src, wres, rres=(), free=False):
            S.dma(sp_q, lambda e: e.dma_start(out=dst, in_=src), reads=list(rres), writes=list(wres), free=free)

        def rstd_from_ss(col0, ncols, n, rs):
            a = stat[:, col0:col0 + ncols]
            S.op("act", lambda e: e.activation(a, a, AF.Ln, bias=EPS, scale=1.0 / n), reads=[rs], writes=[rs])
            S.op("act", lambda e: e.activation(a, a, AF.Exp, scale=-0.5), reads=[rs], writes=[rs])

        def transpose_to(ps_ap, src_ap, res_src, res_ps):
            S.op("pe", lambda e: e.matmul(ps_ap, lhsT=src_ap, rhs=ident, start=True, stop=True),
                 reads=[res_src, rCB], writes=[res_ps])

        def norm_and_transpose(j, bufs=None, pre=None):
            if bufs is None:
                bufs = (hT, rhT, xn[:], rxn, hbf[:], rhbf, 0, rstat)
            hT_, rhT_, xn_, rxn_, hbf_, rhbf_, sc_, rst_ = bufs
            for t in range(4):
                tg = 4 * j + t
                xt = X[:, tg, :]
                if pre is None:
                    S.op("act", lambda e, xt=xt: e.activation(hbf_, xt, AF.Square, accum_out=stat[:, sc_:sc_ + 1]),
                         reads=[rX[tg]], writes=[rhbf_, rst_])
                    rstd_from_ss(sc_, 1, D, rst_)
                    rcol, rres = stat[:, sc_:sc_ + 1], rst_
                else:
                    rcol, rres = pre[0][:, tg:tg + 1], pre[1]
                S.op("dve", lambda e, xt=xt, rcol=rcol: e.scalar_tensor_tensor(xn_, xt, rcol, M0[:], ALU.mult, ALU.mult),
                     reads=[rX[tg], rres, rM0], writes=[rxn_])
                S.op("dve", lambda e: e.tensor_tensor(hbf_, xn_, M1[:], ALU.add), reads=[rxn_, rM1], writes=[rhbf_])
                for g in range(2):
                    pt, rp = rot4.next()
                    for c in range(4):
                        kc = 4 * g + c
                        transpose_to(pt[:, c * 128:(c + 1) * 128], hbf_[:, kc * 128:(kc + 1) * 128], rhbf_, rp)
                    S.op("act", lambda e, pt=pt, g=g, t=t: e.copy(hT_[:, 4 * g:4 * g + 4, t * 128:(t + 1) * 128],
                                                                 pt[:].rearrange("p (c n) -> p c n", c=4)),
                         reads=[rp], writes=[rhT_])
            return hT_, rhT_

        def load_mods(l, s, which, dst, rdst):
            load(dst[:], mods_d[l, s, which * D:(which + 1) * D].partition_broadcast(128), [rdst], [r_mods])

        def setup_norm_mods(l, s, sub):
            gsrc = norm1_d if sub == 0 else norm2_d
            load(M1[:], gsrc[l, :].partition_broadcast(128), [rM1])
            load_mods(l, s, 3 * sub + 1, M0, rM0)
            S.op("dve", lambda e: e.scalar_tensor_tensor(M0[:], M0[:], 1.0, M1[:], ALU.add, ALU.mult),
                 reads=[rM0, rM1], writes=[rM0])
            load_mods(l, s, 3 * sub + 0, M1, rM1)

        def residual_update(ps_ap, rp, tg, half, gate, rgate, tmp, rtmp):
            S.op("dve", lambda e: e.tensor_tensor(tmp, ps_ap, gate[:, half * 512:(half + 1) * 512], ALU.mult),
                 reads=[rp, rgate], writes=[rtmp])
            xa = X[:, tg, half * 512:(half + 1) * 512]
            S.op("dve", lambda e: e.tensor_tensor(xa, xa, tmp, ALU.add), reads=[rtmp, rX[tg]], writes=[rX[tg]])

        load(CB[:], cbf_d, [rCB])
        load(CF[:], cf_d, [rCF])
        rZT = Res("zeros")
        S.warm_dep = S.op("dve", lambda e: e.memset(ZT[:], 0.0), writes=[rZT])
        if WARM_FILL[0] > 0:
            S.warm_fn = lambda e: e.matmul(PS[WARM_BANK[0]][0][:, 0:256], lhsT=ZT[:, 0:128], rhs=ZT[:, 128:384], start=True, stop=True)
        r_wb = {}

        def convert(name, dst, src, rows, l, after=()):
            r = Res("wb_%s_%d" % (name, l))
            r_wb[(name, l)] = r
            for r0 in range(0, rows, 128):
                nr = min(128, rows - r0)
                S.dma("pool", lambda e, r0=r0, nr=nr: e.dma_start(out=dst[l, r0:r0 + nr, :], in_=src[l, r0:r0 + nr, :]),
                      reads=list(after), writes=[r], free=True)

        def convert_up(l, after=()):
            r = Res("wb_up_%d" % l)
            r_wb[("up", l)] = r
            for kc in range(8):
                for gv in range(2):
                    for i0 in range(0, NFC, 11):
                        ni = min(11, NFC - i0)
                        c0 = gv * D_FF + i0 * 128
                        src = w_up_d[l, kc * 128:(kc + 1) * 128, c0:c0 + ni * 128].rearrange("p (i n) -> p i n", i=ni)
                        dst = wb_up[l, i0:i0 + ni, :, kc, gv * 128:(gv + 1) * 128].rearrange("i p n -> p i n")
                        S.dma("pool", lambda e, dst=dst, src=src: e.dma_start(out=dst, in_=src),
                              reads=list(after), writes=[r], free=True)

        convert("in", wb_in, w_in_d, D, 0)
        convert("uq", wb_uq, w_uq_d, 256, 0)
        convert("ukv", wb_ukv, w_ukv_d, 128, 0)

        def convert_rest():
            for l in range(DEPTH):
                if l > 0:
                    convert("ada", wb_ada, w_ada_d, D, l, after=[r_mods])
                    convert("in", wb_in, w_in_d, D, l, after=[r_mods])
                    convert("uq", wb_uq, w_uq_d, 256, l, after=[r_mods])
                    convert("ukv", wb_ukv, w_ukv_d, 128, l, after=[r_mods])
                convert("out", wb_out, w_out_d, D, l, after=[r_mods])
                convert_up(l, after=[r_mods])
                convert("down", wb_down, w_down_d, D_FF, l, after=[r_mods])

        r_mods = Res("mods")

        def compute_mods(l):
            cvp = Carve(ATT_BASE)
            cT = cvp.f32([128, 8, NSEQ])
            scT = cvp.bf([128, 8, NSEQ])
            WA_buf = [(cvp.bf([128, 8, 512]), Res("wada%d" % i)) for i in range(3)]
            MR = Rot([(cvp.f32([NSEQ, 512]), Res("modrow%d" % i)) for i in range(2)])
            BA = Rot([(cvp.f32([NSEQ, 512]), Res("badd%d" % i)) for i in range(2)])
            r_pro = Res("pro")
            load(cT, cT_d, [r_pro])
            S.op("act", lambda e: e.activation(scT, cT, AF.Silu), reads=[r_pro], writes=[r_pro])
            wrot = Rot(WA_buf)
            for n in range(12):
                wbuf, rw = wrot.next()
                badd, rba = BA.next()
                modrow, rmr = MR.next()
                load(badd, b_ada_d[l, n * 512:(n + 1) * 512].partition_broadcast(NSEQ), [rba], [])
                if l == 0:
                    S.dma("pool", lambda e, wbuf=wbuf, n=n: e.dma_start(
                        out=wbuf, in_=w_ada_d[l, :, n * 512:(n + 1) * 512].rearrange("(c p) n -> p c n", p=128)), writes=[rw])
                else:
                    load(wbuf, wb_ada[l, :, n * 512:(n + 1) * 512].rearrange("(c p) n -> p c n", p=128), [rw],
                         [r_wb[("ada", l)]])
                pt, rp = rot4.next()
                for kc in range(8):
                    S.op("pe", lambda e, pt=pt, kc=kc, wbuf=wbuf: e.matmul(
                        pt[0:NSEQ, :], lhsT=scT[:, kc, :], rhs=wbuf[:, kc, :], start=(kc == 0), stop=(kc == 7)),
                        reads=[r_pro, rw], writes=[rp])
                S.op("dve", lambda e, pt=pt, modrow=modrow, badd=badd: e.tensor_tensor(modrow, pt[0:NSEQ, :], badd, ALU.add),
                     reads=[rp, rba], writes=[rmr])
                S.dma(sp_q, lambda e, l=l, n=n, modrow=modrow: e.dma_start(out=mods_d[l, :, n * 512:(n + 1) * 512], in_=modrow),
                      reads=[rmr], writes=[r_mods])
            S.barrier()

        def rope_tables(s):
            cv = Carve(ATT_BASE)
            posi = cv.f32([128, NT]).bitcast(I32)
            posf = cv.f32([128, NT])
            ang = cv.f32([128, NT, 32])
            ki = cv.f32([128, NT, 32]).bitcast(I32)
            kf = cv.f32([128, NT, 32])
            r = Res("ropetmp")
            load(posi, posT_d[s], [r])
            S.op("dve", lambda e: e.tensor_copy(posf, posi), reads=[r], writes=[r])
            S.op("dve", lambda e: e.tensor_tensor(ang, posf.unsqueeze(2).to_broadcast([128, NT, 32]),
                                                  CF[:, 0:32].unsqueeze(1).to_broadcast([128, NT, 32]), ALU.mult),
                 reads=[r, rCF], writes=[r])
            S.op("dve", lambda e: e.tensor_tensor(ang, ang, CF[:, 32:64].unsqueeze(1).to_broadcast([128, NT, 32]), ALU.add),
                 reads=[r, rCF], writes=[r])
            S.op("dve", lambda e: e.tensor_scalar(kf, ang, 1.0 / (2 * math.pi), None, ALU.mult), reads=[r], writes=[r])
            S.op("dve", lambda e: e.tensor_copy(ki, kf), reads=[r], writes=[r])
            S.op("dve", lambda e: e.tensor_copy(kf, ki), reads=[r], writes=[r])
            S.op("dve", lambda e: e.scalar_tensor_tensor(ang, kf, -2 * math.pi, ang, ALU.mult, ALU.add), reads=[r], writes=[r])
            S.op("dve", lambda e: e.tensor_scalar(kf, ang, math.pi, -2 * math.pi, ALU.is_gt, ALU.mult), reads=[r], writes=[r])
            S.op("dve", lambda e: e.tensor_tensor(ang, ang, kf, ALU.add), reads=[r], writes=[r])
            S.op("dve", lambda e: e.tensor_scalar(kf, ang, -math.pi, 2 * math.pi, ALU.is_lt, ALU.mult), reads=[r], writes=[r])
            S.op("dve", lambda e: e.tensor_tensor(ang, ang, kf, ALU.add), reads=[r], writes=[r])
            S.op("act", lambda e: e.activation(SC[:], ang, AF.Sin), reads=[r], writes=[rSC])
            S.barrier()

        def pass_A(l, s, cv):
            W = cv.bf([128, 8, 768]); rW = Res("WA")
            QT = cv.bf([128, 2, 512]); rQT = Res("QTsb")
            KT = cv.bf([128, 2, SEQ]); rKT = [Res("KTsb%d" % j) for j in range(NSB)]
            V = cv.bf([128, NT, 256]); rV = [Res("Vsb%d" % j) for j in range(NSB)]
            Eb = Rot([(cv.f32([128, 512]), Res("E%d" % i)) for i in range(DEPTHS["E"])])
            SPb = Rot([(cv.bf([128, 512]), Res("SP%d" % i)) for i in range(DEPTHS["SP"])])
            TMb = Rot([(cv.f32([128, 512]), Res("TM%d" % i)) for i in range(DEPTHS["TM"])])
            Wb = Rot([(cv.bf([128, 512]), Res("Wt%d" % i)) for i in range(DEPTHS["W"])])
            SPS = cv.bf([128, 512]); rSPS = Res("SPS")
            load(W, wb_in[l, :, 0:768].rearrange("(c p) n -> p c n", p=128), [rW], [r_wb[("in", l)]])
            Oacc, rO = PS[4]

            def body(j):
                for which in range(2):
                    for p in range(2):
                        pt, rp = rot4.next()
                        c0 = which * 256 + p * 128
                        for kc in range(8):
                            S.op("pe", lambda e, pt=pt, kc=kc, c0=c0: e.matmul(
                                pt[:], lhsT=W[:, kc, c0:c0 + 128], rhs=hT[:, kc, :], start=(kc == 0), stop=(kc == 7)),
                                reads=[rW, rhT], writes=[rp])
                        if which == 0:
                            S.op("act", lambda e, pt=pt, p=p: e.copy(QT[:, p, :], pt[:]), reads=[rp], writes=[rQT])
                        else:
                            S.op("act", lambda e, pt=pt, p=p, j=j: e.copy(KT[:, p, j * 512:(j + 1) * 512], pt[:]),
                                 reads=[rp], writes=[rKT[j]])
                for t in range(4):
                    pt, rp = rot4.next()
                    for kc in range(8):
                        S.op("pe", lambda e, pt=pt, kc=kc, t=t: e.matmul(
                            pt[:, 0:256], lhsT=hT[:, kc, t * 128:(t + 1) * 128], rhs=W[:, kc, 512:768],
                            start=(kc == 0), stop=(kc == 7)), reads=[rW, rhT], writes=[rp])
                    S.op("act", lambda e, pt=pt, t=t, j=j: e.copy(V[:, 4 * j + t, :], pt[:, 0:256]),
                         reads=[rp], writes=[rV[j]])
                yield "proj"
                for p in range(2):
                    for hh in range(2):
                        h = 2 * p + hh
                        b0 = 64 * hh
                        S.op("dve", lambda e: e.memset(SPS, 0.0), writes=[rSPS])
                        nkb = 4 * j + 4
                        for kb in range(nkb - 1, -1, -1):
                            first = (kb == nkb - 1)
                            qlo = max(0, kb - 4 * j)
                            c0 = qlo * 128
                            jk = kb // 4
                            Z, rZ = rot4.next()
                            S.op("pe", lambda e, Z=Z, b0=b0, p=p, kb=kb, c0=c0: e.matmul(
                                Z[:, c0:512], lhsT=KT[b0:b0 + 64, p, kb * 128:(kb + 1) * 128], rhs=QT[b0:b0 + 64, p, c0:512],
                                start=True, stop=True), reads=[rKT[jk], rQT], writes=[rZ])
                            E, rE = Eb.next()
                            S.op("act", lambda e, E=E, Z=Z, c0=c0: e.activation(E[:, c0:512], Z[:, c0:512], AF.Exp, scale=0.125),
                                 reads=[rZ], writes=[rE])
                            SP, rSP = SPb.next()
                            S.op("act", lambda e, E=E, SP=SP, c0=c0: e.activation(SP[:, c0:512], E[:, c0:512], AF.Ln, bias=1.0),
                                 reads=[rE], writes=[rSP])
                            if kb >= 4 * j:
                                S.op("dve", lambda e, SP=SP, c0=c0: e.tensor_tensor(SP[:, c0:c0 + 128], SP[:, c0:c0 + 128], maskS, ALU.mult),
                                     reads=[rSP, rCB], writes=[rSP])
                            C, rC = rot4.next()
                            S.op("pe", lambda e, C=C, SP=SP, c0=c0, first=first: e.matmul(C[:, c0:512], lhsT=negU, rhs=SP[:, c0:512], start=True, stop=first),
                                 reads=[rSP, rCB], writes=[rC])
                            if not first:
                                S.op("pe", lambda e, C=C, c0=c0: e.matmul(C[:, c0:512], lhsT=negOnes, rhs=SPS[:, c0:512], start=False, stop=True),
                                     reads=[rSPS, rCB], writes=[rC])
                            G, rG_ = TMb.next()
                            S.op("act", lambda e, G=G, C=C, c0=c0: e.activation(G[:, c0:512], C[:, c0:512], AF.Exp),
                                 reads=[rC], writes=[rG_])
                            if kb > 0:
                                S.op("dve", lambda e, SP=SP, c0=c0: e.tensor_tensor(SPS[:, c0:512], SPS[:, c0:512], SP[:, c0:512], ALU.add),
                                     reads=[rSP, rSPS], writes=[rSPS])
                            Wt, rWt = Wb.next()
                            if first and c0 > 0:
                                S.op("dve", lambda e, Wt=Wt, c0=c0: e.memset(Wt[:, 0:c0], 0.0), writes=[rWt])
                            S.op("dve", lambda e, Wt=Wt, E=E, G=G, c0=c0: e.tensor_tensor(Wt[:, c0:512], E[:, c0:512], G[:, c0:512], ALU.mult),
                                 reads=[rE, rG_], writes=[rWt])
                            if kb >= 4 * j:
                                S.op("dve", lambda e, Wt=Wt, c0=c0: e.tensor_tensor(Wt[:, c0:c0 + 128], Wt[:, c0:c0 + 128], maskS, ALU.mult),
                                     reads=[rWt, rCB], writes=[rWt])
                            pc0 = 0 if first else c0
                            S.op("pe", lambda e, Wt=Wt, b0=b0, kb=kb, h=h, pc0=pc0, first=first: e.matmul(
                                Oacc[b0:b0 + 64, pc0:512], lhsT=V[:, kb, h * 64:(h + 1) * 64], rhs=Wt[:, pc0:512],
                                start=first, stop=(kb == 0)), reads=[rWt, rV[jk]], writes=[rO])
                            yield "it"
                    S.op("act", lambda e, p=p, j=j: e.copy(mixT[:, p, j * 512:(j + 1) * 512], Oacc[:]), reads=[rO], writes=[rmix[j]])
            return body

        def pass_B(l, s):
            cv = Carve(ATT_BASE)
            W = cv.bf([128, 8, 416]); rW = Res("WB")
            WUQ = cv.bf([128, 2, 576]); WUKV = cv.bf([128, 768])
            QT = cv.bf([128, 6, 512]); rQT = Res("QTm")
            KT = cv.bf([128, 6, SEQ]); rKT = [Res("KTm%d" % j) for j in range(NSB)]
            V = cv.bf([128, NT, 384]); rV = [Res("Vm%d" % j) for j in range(NSB)]
            lat = cv.f32([128, 416]); rlat = Res("lat")
            cqn = cv.bf([128, 384]); rcqn = Res("cqn")
            latT = cv.bf([128, 3, 128]); rlatT = Res("latT")
            qf = cv.f32([128, 6, 96]); rqf = Res("qf")
            kf = cv.f32([128, 6, 96]); rkf = Res("kf")
            sq = cv.f32([128, 6, 96]); rsq = Res("sq")
            rt = cv.f32([128, 6, 4, 16]); rrt = Res("rt")
            q16 = cv.bf([128, 6, 96]); rq16 = Res("q16")
            k16 = cv.bf([128, 6, 96]); rk16 = Res("k16")
            Pb = Rot([(cv.bf([128, 512]), Res("P%d" % i)) for i in range(3)])
            rec = cv.f32([128, 512]); rrec = Res("rec")
            load(W, wb_in[l, :, 768:1184].rearrange("(c p) n -> p c n", p=128), [rW], [r_wb[("in", l)]])
            load(WUQ, wb_uq[l].rearrange("(c p) n -> p c n", p=128), [rW], [r_wb[("uq", l)]])
            load(WUKV, wb_ukv[l], [rW], [r_wb[("ukv", l)]])
            load(G_cq[:], cqg_d[l, :].partition_broadcast(128), [rG])
            load(G_ckv[:], ckvg_d[l, :].partition_broadcast(128), [rG])
            load(G_q[:], qng_d[l, :].partition_broadcast(128), [rG])
            load(G_k[:], kng_d[l, :].partition_broadcast(128), [rG])
            Oacc, rO = PS[4]
            Dacc, rD = PS[5]

            def qk_norm_rope(src, rsrc, gain, dst16, rdst, tg):
                S.op("dve", lambda e: e.tensor_tensor(sq, src, src, ALU.mult), reads=[rsrc], writes=[rsq])
                S.op("dve", lambda e: e.tensor_reduce(stat[:, 8:14], sq, axis=AX.X, op=ALU.add), reads=[rsq], writes=[rstat8])
                rstd_from_ss(8, 6, 96, rstat8)
                S.op("dve", lambda e: e.tensor_tensor(src, src, stat[:, 8:14].unsqueeze(2).to_broadcast([128, 6, 96]), ALU.mult),
                     reads=[rsrc, rstat8], writes=[rsrc])
                S.op("dve", lambda e: e.tensor_tensor(src, src, gain[:].unsqueeze(1).to_broadcast([128, 6, 96]), ALU.mult),
                     reads=[rsrc, rG], writes=[rsrc])
                S.op("dve", lambda e: e.tensor_copy(dst16[:, :, 0:64], src[:, :, 0:64]), reads=[rsrc], writes=[rdst])
                sin = SC[:, tg, 0:16].unsqueeze(1).to_broadcast([128, 6, 16])
                cos = SC[:, tg, 16:32].unsqueeze(1).to_broadcast([128, 6, 16])
                x1 = src[:, :, 64:80]
                x2 = src[:, :, 80:96]
                S.op("dve", lambda e: e.tensor_tensor(rt[:, :, 0, :], x1, cos, ALU.mult), reads=[rsrc, rSC], writes=[rrt])
                S.op("dve", lambda e: e.tensor_tensor(rt[:, :, 1, :], x2, sin, ALU.mult), reads=[rsrc, rSC], writes=[rrt])
                S.op("dve", lambda e: e.tensor_tensor(rt[:, :, 2, :], x1, sin, ALU.mult), reads=[rsrc, rSC], writes=[rrt])
                S.op("dve", lambda e: e.tensor_tensor(rt[:, :, 3, :], x2, cos, ALU.mult), reads=[rsrc, rSC], writes=[rrt])
                S.op("dve", lambda e: e.tensor_tensor(dst16[:, :, 64:80], rt[:, :, 0, :], rt[:, :, 1, :], ALU.subtract),
                     reads=[rrt], writes=[rdst])
                S.op("dve", lambda e: e.tensor_tensor(dst16[:, :, 80:96], rt[:, :, 2, :], rt[:, :, 3, :], ALU.add),
                     reads=[rrt], writes=[rdst])

            norm_and_transpose(0)
            for j in range(NSB):
                for t in range(4):
                    tg = 4 * j + t
                    pt, rp = rot4.next()
                    for kc in range(8):
                        S.op("pe", lambda e, pt=pt, kc=kc, t=t: e.matmul(
                            pt[:, 0:416], lhsT=hT[:, kc, t * 128:(t + 1) * 128], rhs=W[:, kc, :],
                            start=(kc == 0), stop=(kc == 7)), reads=[rW, rhT], writes=[rp])
                    S.op("act", lambda e, pt=pt: e.copy(lat, pt[:, 0:416]), reads=[rp], writes=[rlat])
                    if DBG_CUT[0] == 1:
                        S.barrier(); return
                    S.op("act", lambda e: e.activation(sq[:, 0:3, :].rearrange("p a b -> p (a b)")[:, 0:256], lat[:, 0:256], AF.Square,
                                                       accum_out=stat[:, 4:5]), reads=[rlat], writes=[rsq, rstat4])
                    rstd_from_ss(4, 1, 256, rstat4)
                    S.op("act", lambda e: e.activation(sq[:, 0:3, :].rearrange("p a b -> p (a b)")[:, 0:128], lat[:, 256:384], AF.Square,
                                                       accum_out=stat[:, 5:6]), reads=[rlat], writes=[rsq, rstat5])
                    rstd_from_ss(5, 1, 128, rstat5)
                    S.op("dve", lambda e: e.scalar_tensor_tensor(cqn[:, 0:256], lat[:, 0:256], stat[:, 4:5], G_cq[:], ALU.mult, ALU.mult),
                         reads=[rlat, rstat4, rG], writes=[rcqn])
                    S.op("dve", lambda e: e.scalar_tensor_tensor(cqn[:, 256:384], lat[:, 256:384], stat[:, 5:6], G_ckv[:], ALU.mult, ALU.mult),
                         reads=[rlat, rstat5, rG], writes=[rcqn])
                    pt, rp = rot4.next()
                    for c in range(3):
                        transpose_to(pt[:, c * 128:(c + 1) * 128], cqn[:, c * 128:(c + 1) * 128], rcqn, rp)
                    S.op("act", lambda e, pt=pt: e.copy(latT, pt[:, 0:384].rearrange("p (c n) -> p c n", c=3)), reads=[rp], writes=[rlatT])
                    if DBG_CUT[0] == 2:
                        S.barrier(); return
                    for half in range(2):
                        pt, rp = rot4.next()
                        for c in range(2):
                            S.op("pe", lambda e, pt=pt, c=c, half=half: e.matmul(
                                pt[:, 0:288], lhsT=latT[:, c, :], rhs=WUQ[:, c, half * 288:(half + 1) * 288],
                                start=(c == 0), stop=(c == 1)), reads=[rlatT, rW], writes=[rp])
                        S.op("act", lambda e, pt=pt, half=half: e.copy(qf[:, 3 * half:3 * half + 3, :],
                                                                     pt[:, 0:288].rearrange("p (a b) -> p a b", a=3)),
                             reads=[rp], writes=[rqf])
                    if DBG_CUT[0] == 6:
                        S.barrier(); return
                    for half in range(2):
                        pt, rp = rot4.next()
                        S.op("pe", lambda e, pt=pt, half=half: e.matmul(
                            pt[:, 0:384], lhsT=latT[:, 2, :], rhs=WUKV[:, half * 384:(half + 1) * 384], start=True, stop=True),
                            reads=[rlatT, rW], writes=[rp])
                        pv = pt[:, 0:384].rearrange("p (a b) -> p a b", a=3)
                        S.op("act", lambda e, pv=pv, half=half: e.copy(kf[:, 3 * half:3 * half + 3, 0:64], pv[:, :, 0:64]),
                             reads=[rp], writes=[rkf])
                        if DBG_CUT[0] == 7:
                            continue
                        S.op("dve", lambda e, pv=pv, half=half, tg=tg: e.tensor_copy(
                            V[:, tg, half * 192:(half + 1) * 192].rearrange("p (a b) -> p a b", a=3), pv[:, :, 64:128]),
                            reads=[rp], writes=[rV[j]])
                    if DBG_CUT[0] in (7, 8):
                        S.barrier(); return
                    S.op("dve", lambda e: e.tensor_copy(kf[:, :, 64:96], lat[:, 384:416].unsqueeze(1).to_broadcast([128, 6, 32])),
                         reads=[rlat], writes=[rkf])
                    if DBG_CUT[0] == 3:
                        S.barrier(); return
                    qk_norm_rope(qf, rqf, G_q, q16, rq16, tg)
                    if DBG_CUT[0] == 4:
                        S.barrier(); return
                    qk_norm_rope(kf, rkf, G_k, k16, rk16, tg)
                    for (src16, rsrc16, isq) in ((q16, rq16, True), (k16, rk16, False)):
                        for (h0, nh) in ((0, 4), (4, 2)):
                            pt, rp = rot4.next()
                            for hh in range(nh):
                                transpose_to(pt[0:96, hh * 128:(hh + 1) * 128], src16[:, h0 + hh, :], rsrc16, rp)
                            pv = pt[0:96, 0:nh * 128].rearrange("p (a b) -> p a b", a=nh)
                            if isq:
                                S.op("act", lambda e, pv=pv, h0=h0, nh=nh, t=t: e.copy(QT[0:96, h0:h0 + nh, t * 128:(t + 1) * 128], pv),
                                     reads=[rp], writes=[rQT])
                            else:
                                S.op("act", lambda e, pv=pv, h0=h0, nh=nh, tg=tg: e.copy(KT[0:96, h0:h0 + nh, tg * 128:(tg + 1) * 128], pv),
                                     reads=[rp], writes=[rKT[j]])
                if j + 1 < NSB:
                    norm_and_transpose(j + 1)
                sc = 1.0 / math.sqrt(96.0)
                for p in range(3):
                    for hh in range(2):
                        h = 2 * p + hh
                        b0 = 64 * hh
                        nkb = 4 * j + 4
                        for kb in range(nkb):
                            qlo = max(0, kb - 4 * j)
                            c0 = qlo * 128
                            jk = kb // 4
                            Z, rZ = rot4.next()
                            S.op("pe", lambda e, Z=Z, h=h, kb=kb, c0=c0: e.matmul(
                                Z[:, c0:512], lhsT=KT[0:96, h, kb * 128:(kb + 1) * 128], rhs=QT[0:96, h, c0:512],
                                start=True, stop=True), reads=[rKT[jk], rQT], writes=[rZ])
                            P, rP = Pb.next()
                            S.op("act", lambda e, P=P, Z=Z, c0=c0: e.activation(P[:, c0:512], Z[:, c0:512], AF.Exp, scale=sc),
                                 reads=[rZ], writes=[rP])
                            if kb >= 4 * j:
                                S.op("dve", lambda e, P=P, c0=c0: e.tensor_tensor(P[:, c0:c0 + 128], P[:, c0:c0 + 128], maskC, ALU.mult),
                                     reads=[rP, rCB], writes=[rP])
                            S.op("pe", lambda e, P=P, b0=b0, kb=kb, h=h, c0=c0, nkb=nkb: e.matmul(
                                Oacc[b0:b0 + 64, c0:512], lhsT=V[:, kb, h * 64:(h + 1) * 64], rhs=P[:, c0:512],
                                start=(kb == 0), stop=(kb == nkb - 1)), reads=[rP, rV[jk]], writes=[rO])
                            S.op("pe", lambda e, P=P, b0=b0, kb=kb, c0=c0, nkb=nkb: e.matmul(
                                Dacc[b0:b0 + 64, c0:512], lhsT=ones[:, 0:64], rhs=P[:, c0:512],
                                start=(kb == 0), stop=(kb == nkb - 1)), reads=[rP, rCB], writes=[rD])
                    S.op("dve", lambda e: e.reciprocal(rec, Dacc[:]), reads=[rD], writes=[rrec])
                    S.op("dve", lambda e, p=p, j=j: e.tensor_tensor(mixT[:, 2 + p, j * 512:(j + 1) * 512], Oacc[:], rec, ALU.mult),
                         reads=[rO, rrec], writes=[rmix[j]])
            S.barrier()

        def pass_C(l, s, cv):
            W = cv.bf([128, 8, 640]); rW = Res("WC")
            QT = cv.bf([128, 3, 512]); rQT = Res("QTs")
            KT = cv.bf([128, SEQ]); rKT = [Res("KTs%d" % j) for j in range(NSB)]
            V = cv.bf([128, NT, 128]); rV = [Res("Vs%d" % j) for j in range(NSB)]
            qk = cv.f32([128, 8, 64]); rqk = Res("qk")
            sq = cv.f32([128, 8, 64]); rsq = Res("sqs")
            stg = cv.bf([128, 4, 128]); rstg = Res("stg")
            Pm = Rot([(cv.bf([128, 2, 384]), Res("Pm%d" % i)) for i in range(2)])
            rec = cv.f32([128, 384]); rrec = Res("recs")
            load(W, wb_in[l, :, 1184:1824].rearrange("(c p) n -> p c n", p=128), [rW], [r_wb[("in", l)]])
            load(G_sw[:].rearrange("p a b -> p (a b)"), swg_d[l, :].partition_broadcast(128), [rG])
            load(ESINK[:], sink_d[l], [rG])
            S.op("act", lambda e: e.activation(ESINK[:], ESINK[:], AF.Exp), reads=[rG], writes=[rG])
            Oacc, rO = PS[5]
            Dacc, rD = PS[7]

            def body(j):
                for t in range(4):
                    tg = 4 * j + t
                    p1, rp1 = rot4.next()
                    for kc in range(8):
                        S.op("pe", lambda e, p1=p1, kc=kc, t=t: e.matmul(
                            p1[:], lhsT=hT[:, kc, t * 128:(t + 1) * 128], rhs=W[:, kc, 0:512],
                            start=(kc == 0), stop=(kc == 7)), reads=[rW, rhT], writes=[rp1])
                    p2, rp2 = rot4.next()
                    for kc in range(8):
                        S.op("pe", lambda e, p2=p2, kc=kc, t=t: e.matmul(
                            p2[:, 0:128], lhsT=hT[:, kc, t * 128:(t + 1) * 128], rhs=W[:, kc, 512:640],
                            start=(kc == 0), stop=(kc == 7)), reads=[rW, rhT], writes=[rp2])
                    S.op("act", lambda e, p1=p1: e.copy(qk, p1[:].rearrange("p (a b) -> p a b", a=8)), reads=[rp1], writes=[rqk])
                    S.op("act", lambda e, p2=p2, tg=tg: e.copy(V[:, tg, :], p2[:, 0:128]), reads=[rp2], writes=[rV[j]])
                    S.op("dve", lambda e: e.tensor_tensor(sq, qk, qk, ALU.mult), reads=[rqk], writes=[rsq])
                    S.op("dve", lambda e: e.tensor_reduce(stat[:, 16:24], sq, axis=AX.X, op=ALU.add), reads=[rsq], writes=[rstat16])
                    rstd_from_ss(16, 8, 64, rstat16)
                    S.op("dve", lambda e: e.tensor_tensor(qk, qk, stat[:, 16:24].unsqueeze(2).to_broadcast([128, 8, 64]), ALU.mult),
                         reads=[rqk, rstat16], writes=[rqk])
                    S.op("dve", lambda e: e.tensor_tensor(
                        stg[:, 0:3, :].rearrange("p r (g d) -> p r g d", g=2),
                        qk[:, 0:6, :].rearrange("p (g r) d -> p r g d", g=2),
                        G_sw[:, 0, :].unsqueeze(1).unsqueeze(1).to_broadcast([128, 3, 2, 64]), ALU.mult),
                        reads=[rqk, rG], writes=[rstg])
                    S.op("dve", lambda e: e.tensor_tensor(stg[:, 3, :].rearrange("p (g d) -> p g d", g=2), qk[:, 6:8, :], G_sw[:, 1, :].unsqueeze(1).to_broadcast([128, 2, 64]), ALU.mult),
                         reads=[rqk, rG], writes=[rstg])
                    pt, rp = rot4.next()
                    for c in range(4):
                        transpose_to(pt[:, c * 128:(c + 1) * 128], stg[:, c, :], rstg, rp)
                    S.op("act", lambda e, pt=pt, t=t: e.copy(QT[:, :, t * 128:(t + 1) * 128], pt[:, 0:384].rearrange("p (a b) -> p a b", a=3)),
                         reads=[rp], writes=[rQT])
                    S.op("act", lambda e, pt=pt, tg=tg: e.copy(KT[:, tg * 128:(tg + 1) * 128], pt[:, 384:512]), reads=[rp], writes=[rKT[j]])
                yield "proj"
                for qi in range(4):
                    n = 4 * j + qi
                    blks = [1] if n == 0 else [0, 1]
                    for g in range(2):
                        b0 = 64 * g
                        Zs = {}
                        for blk in blks:
                            kbi = n - 1 + blk
                            Z, rZ = rot4.next()
                            Zs[blk] = (Z, rZ)
                            S.op("pe", lambda e, Z=Z, b0=b0, kbi=kbi, qi=qi: e.matmul(
                                Z[:, 0:384].rearrange("p (a b) -> p a b", a=3), lhsT=KT[b0:b0 + 64, kbi * 128:(kbi + 1) * 128],
                                rhs=QT[b0:b0 + 64, :, qi * 128:(qi + 1) * 128], start=True, stop=True),
                                reads=[rKT[kbi // 4], rQT], writes=[rZ])
                        PM, rPM = Pm.next()
                        for blk in blks:
                            Z, rZ = Zs[blk]
                            S.op("act", lambda e, Z=Z, blk=blk, PM=PM: e.activation(PM[:, blk, :], Z[:, 0:384], AF.Exp, scale=0.125),
                                 reads=[rZ], writes=[rPM])
                        for blk in blks:
                            S.op("dve", lambda e, PM=PM, blk=blk, g=g: e.tensor_tensor(
                                PM[:, blk, :].rearrange("p (r q) -> p r q", r=3), PM[:, blk, :].rearrange("p (r q) -> p r q", r=3),
                                ESW[:, 3 * g:3 * g + 3, blk, :], ALU.mult), reads=[rPM, rESW], writes=[rPM])
                        for r in range(3):
                            h = 3 * g + r
                            pr = h // 2
                            ob = 64 * (h % 2)
                            for bi, blk in enumerate(blks):
                                kbi = n - 1 + blk
                                S.op("pe", lambda e, PM=PM, ob=ob, pr=pr, kbi=kbi, g=g, blk=blk, r=r, bi=bi, nb=len(blks): e.matmul(
                                    Oacc[ob:ob + 64, pr * 128:(pr + 1) * 128], lhsT=V[:, kbi, g * 64:(g + 1) * 64],
                                    rhs=PM[:, blk, r * 128:(r + 1) * 128], start=(bi == 0), stop=(bi == nb - 1)),
                                    reads=[rPM, rV[kbi // 4]], writes=[rO])
                                S.op("pe", lambda e, PM=PM, ob=ob, pr=pr, blk=blk, r=r, bi=bi, nb=len(blks): e.matmul(
                                    Dacc[ob:ob + 64, pr * 128:(pr + 1) * 128], lhsT=ones[:, 0:64],
                                    rhs=PM[:, blk, r * 128:(r + 1) * 128], start=(bi == 0), stop=(bi == nb - 1)),
                                    reads=[rPM, rCB], writes=[rD])
                        yield "it"
                    S.op("dve", lambda e: e.tensor_tensor(rec.rearrange("p (a b) -> p a b", a=3), Dacc[:, 0:384].rearrange("p (a b) -> p a b", a=3),
                                                          ESINK[:].unsqueeze(2).to_broadcast([128, 3, 128]), ALU.add),
                         reads=[rD, rG], writes=[rrec])
                    S.op("dve", lambda e: e.reciprocal(rec, rec), reads=[rrec], writes=[rrec])
                    S.op("dve", lambda e, n=n: e.tensor_tensor(mixT[:, 5:8, n * 128:(n + 1) * 128], Oacc[:, 0:384].rearrange("p (a b) -> p a b", a=3),
                                                               rec.rearrange("p (a b) -> p a b", a=3), ALU.mult),
                         reads=[rO, rrec], writes=[rmix[j]])
            return body

        def out_proj(l, s):
            cv = Carve(ATT_BASE)
            WO = cv.bf([128, 8, D]); rWO = Res("WO")
            M2 = cv.f32([128, D]); rM2 = Res("M2")
            tmpb = Rot([(cv.f32([128, 512]), Res("tmpo%d" % i)) for i in range(2)])
            load(WO, wb_out[l].rearrange("(c p) n -> p c n", p=128), [rWO], [r_wb[("out", l)]])
            load_mods(l, s, 2, M2, rM2)
            if dbg and "mixT" in dbg and l == 0 and s == 0:
                for c in range(8):
                    tmp, rtmp = tmpb.next()
                    for q0 in range(0, SEQ, 512):
                        S.op("dve", lambda e, tmp=tmp, c=c, q0=q0: e.tensor_copy(tmp, mixT[:, c, q0:q0 + 512]), reads=rmix, writes=[rtmp])
                        S.dma(sp_q, lambda e, tmp=tmp, c=c, q0=q0: e.dma_start(out=dbg_outs["mixT"][c * 128:(c + 1) * 128, q0:q0 + 512], in_=tmp),
                              reads=[rtmp])
            for tg in range(NT):
                for half in range(2):
                    pt, rp = rot4.next()
                    for c in range(8):
                        S.op("pe", lambda e, pt=pt, c=c, tg=tg, half=half: e.matmul(
                            pt[:], lhsT=mixT[:, c, tg * 128:(tg + 1) * 128], rhs=WO[:, c, half * 512:(half + 1) * 512],
                            start=(c == 0), stop=(c == 7)), reads=[rmix[tg // 4], rWO], writes=[rp])
                    tmp, rtmp = tmpb.next()
                    residual_update(pt[:], rp, tg, half, M2, rM2, tmp, rtmp)
            S.barrier()

        def ffn(l, s, last):
            cv = Carve(0)
            aT = cv.bf([128, NFC, 512]); raT = Res("aT")
            WU = Rot([(cv.bf([128, 8, 256]), Res("WU%d" % i)) for i in range(4)])
            WD = Rot([(cv.bf([128, D]), Res("WD%d" % i)) for i in range(6)])
            UB = [Rot([(cv.f32([128, 514]), Res("UB%d_%d" % (gv, i))) for i in range(3)]) for gv in range(2)]
            ACC = [Rot([(cv.f32([128, 512]), Res("ACC%d_%d" % (gv, i))) for i in range(3)]) for gv in range(2)]
            SG = Rot([(cv.f32([128, 512]), Res("SG%d" % i)) for i in range(3)])
            hT2 = cv.bf([128, 8, 512]); rhT2 = Res("hT2")
            xn2 = cv.f32([128, D]); rxn2 = Res("xn2")
            hbf2 = cv.bf([128, D]); rhbf2 = Res("hbf2")
            NB = Rot([(hT, rhT, xn[:], rxn, hbf[:], rhbf, 0, rstat), (hT2, rhT2, xn2, rxn2, hbf2, rhbf2, 1, rstat1)])
            halo = cv.f32([128, 44, 2]); rhalo = [Res("halo%d" % c) for c in range(44)]
            M2 = cv.f32([128, D]); rM2 = Res("M2f")
            tmpb = Rot([(cv.f32([128, 512]), Res("tmpf%d" % i)) for i in range(2)])
            load(CW[:], cw_d[l], [rG])
            load(CBI[:], cbias_d[l], [rG])
            load_mods(l, s, 5, M2, rM2)
            S.op("dve", lambda e: e.memset(halo, 0.0), writes=rhalo)
            rs_all = cv.f32([128, NT]); rrs = Res("rs_all")
            for tg in range(NT):
                S.op("act", lambda e, tg=tg: e.activation(hbf2, X[:, tg, :], AF.Square, accum_out=rs_all[:, tg:tg + 1]),
                     reads=[rX[tg]], writes=[rhbf2, rrs])
            S.op("act", lambda e: e.activation(rs_all, rs_all, AF.Ln, bias=EPS, scale=1.0 / D), reads=[rrs], writes=[rrs])
            S.op("act", lambda e: e.activation(rs_all, rs_all, AF.Exp, scale=-0.5), reads=[rrs], writes=[rrs])
            pre = (rs_all, rrs)
            hnext = norm_and_transpose(0, NB.next(), pre)
            for j in range(NSB):
                hTc, rhTc = hnext
                for i in range(NFC):
                    wu, rwu = WU.next()
                    load(wu, wb_up[l, i], [rwu], [r_wb[("up", l)]])
                    accs = []
                    for gv in range(2):
                        ch = gv * NFC + i
                        pt, rp = rot4.next()
                        for kc in range(8):
                            S.op("pe", lambda e, pt=pt, kc=kc, wu=wu, gv=gv, hTc=hTc: e.matmul(
                                pt[:], lhsT=wu[:, kc, gv * 128:(gv + 1) * 128], rhs=hTc[:, kc, :], start=(kc == 0), stop=(kc == 7)),
                                reads=[rwu, rhTc], writes=[rp])
                        ub, rub = UB[gv].next()
                        acc, racc = ACC[gv].next()
                        S.op("act", lambda e, pt=pt, ub=ub: e.copy(ub[:, 2:514], pt[:]), reads=[rp], writes=[rub])
                        S.op("act", lambda e, pt=pt, acc=acc, ch=ch: e.activation(acc, pt[:], AF.Identity, bias=CBI[:, ch:ch + 1],
                                                                                 scale=CW[:, ch, 2:3]),
                             reads=[rp, rG], writes=[racc])
                        S.op("dve", lambda e, ub=ub, ch=ch: e.tensor_copy(ub[:, 0:2], halo[:, ch, :]), reads=[rhalo[ch]], writes=[rub])
                        S.op("dve", lambda e, ub=ub, ch=ch: e.tensor_copy(halo[:, ch, :], ub[:, 512:514]), reads=[rub], writes=[rhalo[ch]])
                        S.op("dve", lambda e, ub=ub, acc=acc, ch=ch: e.scalar_tensor_tensor(acc, ub[:, 1:513], CW[:, ch, 1:2], acc, ALU.mult, ALU.add),
                             reads=[rub, racc, rG], writes=[racc])
                        S.op("dve", lambda e, ub=ub, acc=acc, ch=ch: e.scalar_tensor_tensor(acc, ub[:, 0:512], CW[:, ch, 0:1], acc, ALU.mult, ALU.add),
                             reads=[rub, racc, rG], writes=[racc])
                        accs.append((acc, racc))
                    sg, rsg = SG.next()
                    S.op("act", lambda e, sg=sg, acc=accs[0][0]: e.activation(sg, acc, AF.Silu), reads=[accs[0][1]], writes=[rsg])
                    S.op("dve", lambda e, sg=sg, i=i, acc=accs[1][0]: e.tensor_tensor(aT[:, i, :], acc, sg, ALU.mult),
                         reads=[accs[1][1], rsg], writes=[raT])
                if j + 1 < NSB:
                    hnext = norm_and_transpose(j + 1, NB.next(), pre)
                for i in range(NFC):
                    wd, rwd = WD.next()
                    load(wd, wb_down[l, i * 128:(i + 1) * 128, :], [rwd], [r_wb[("down", l)]])
                    for t in range(4):
                        for half in range(2):
                            pt, rp = PS[2 * t + half]
                            S.op("pe", lambda e, pt=pt, wd=wd, i=i, t=t, half=half: e.matmul(
                                pt[:], lhsT=aT[:, i, t * 128:(t + 1) * 128], rhs=wd[:, half * 512:(half + 1) * 512],
                                start=(i == 0), stop=(i == NFC - 1)), reads=[raT, rwd], writes=[rp])
                for t in range(4):
                    tg = 4 * j + t
                    for half in range(2):
                        pt, rp = PS[2 * t + half]
                        tmp, rtmp = tmpb.next()
                        residual_update(pt[:], rp, tg, half, M2, rM2, tmp, rtmp)
                    if last:
                        S.dma(sp_q, lambda e, tg=tg: e.dma_start(out=out_d[s, tg * 128:(tg + 1) * 128, :], in_=X[:, tg, :]),
                              reads=[rX[tg]], free=True)
            S.barrier()

        def setup_esw():
            cv = Carve(ATT_BASE)
            ESWf = cv.f32([128, 6, 2, 128]); rESWf = Res("ESWf")
            load(ESWf.rearrange("p a b c -> p a (b c)"), relg_d, [rESWf])
            S.op("act", lambda e: e.activation(ESWf, ESWf, AF.Exp), reads=[rESWf], writes=[rESWf])
            S.op("dve", lambda e: e.tensor_tensor(ESW, ESWf, CF[:, 64:320].rearrange("p (b c) -> p b c", b=2).unsqueeze(1).to_broadcast([128, 6, 2, 128]),
                                                  ALU.mult), reads=[rESWf, rCF], writes=[rESW])
            S.barrier()

        setup_esw()

        def dump_x(s):
            for tg in range(NT):
                S.dma(sp_q, lambda e, tg=tg: e.dma_start(out=out_d[s, tg * 128:(tg + 1) * 128, :], in_=X[:, tg, :]), reads=[rX[tg]])

        for s in range(NSEQ):
            for tg in range(NT):
                load(X[:, tg, :], x_d[s, tg * 128:(tg + 1) * 128, :], [rX[tg]], free=True)
            if stop_after == "pro":
                dump_x(s); continue
            rope_tables(s)
            if stop_after == "rope":
                dump_x(s); continue
            for l in range(DEPTH):
                if s == 0:
                    compute_mods(l)
                    if l == 0:
                        convert_rest()
                setup_norm_mods(l, s, 0)
                rot4.items = PS[0:4] + ([PS[6]] if WARM_FILL[0] <= 0 else [])
                rot4.i = 0
                S.warm_epochs.add(S.epoch)
                cvac = Carve(ATT_BASE)
                bodyA = pass_A(l, s, cvac)
                bodyC = pass_C(l, s, cvac)
                norm_and_transpose(0)
                for j in range(NSB):
                    gA, gC = bodyA(j), bodyC(j)
                    next(gA)
                    next(gC)
                    if j + 1 < NSB:
                        norm_and_transpose(j + 1)
                    nA, nC = 4 * (4 * j + 4), 8
                    cdone = 0
                    for i in range(nA):
                        next(gA)
                        while cdone < nC and cdone * nA < (i + 1) * nC:
                            next(gC)
                            cdone += 1
                    for g_ in (gA, gC):
                        for _ in g_:
                            pass
                S.barrier()
                rot4.items = PS[0:4] + ([PS[7]] if (WARM_FILL[0] > 0 and WARM_B[0]) else PS[6:8])
                rot4.i = 0
                if stop_after == "A":
                    dump_x(s); break
                if WARM_B[0]:
                    S.warm_epochs.add(S.epoch)
                pass_B(l, s)
                rot4.items = PS[0:4] + PS[6:8]
                rot4.i = 0
                if stop_after == "B":
                    dump_x(s); break
                out_proj(l, s)
                if stop_after == "C":
                    dump_x(s); break
                setup_norm_mods(l, s, 1)
                ffn(l, s, last=(l == DEPTH - 1))
        if SCHEDULE[0]:
            S.schedule()
        else:
            S._resolve_gates()
        if EXPERIMENT[0]:
            return nc, S
        S.emit(sems, dsems)
    return nc, S


def make_in_maps(inputs, n_cores, nseq, seq, depth):
    f = np.float32
    cbf, cf, bidx = _host_consts()
    rel_table = np.asarray(inputs["rel_table"], f)
    relg = np.ascontiguousarray(np.transpose(rel_table[bidx], (0, 3, 1, 2))).reshape(128, 6, 256)
    nt = seq // 128
    sw_g = np.concatenate([np.asarray(inputs["sw_qn_g"], f), np.asarray(inputs["sw_kn_g"], f)], axis=1).reshape(depth, 128)
    sinks = np.asarray(inputs["sw_sinks"], f)
    sinkT = np.zeros((depth, 128, 3), f)
    for pr in range(3):
        sinkT[:, 0:64, pr] = sinks[:, 2 * pr][:, None]
        sinkT[:, 64:128, pr] = sinks[:, 2 * pr + 1][:, None]
    conv_w = np.asarray(inputs["conv_w"], f)
    cwT = np.ascontiguousarray(np.transpose(conv_w.reshape(depth, 3, 44, 128), (0, 3, 2, 1)))
    cbT = np.ascontiguousarray(np.transpose(np.asarray(inputs["conv_b"], f).reshape(depth, 44, 128), (0, 2, 1)))
    shared = {
        "relg": relg, "norm1_g": np.asarray(inputs["norm1_g"], f), "norm2_g": np.asarray(inputs["norm2_g"], f),
        "w_ada": np.asarray(inputs["w_ada"], f), "b_ada": np.asarray(inputs["b_ada"], f),
        "w_in": np.asarray(inputs["w_in"], f), "mla_cq_g": np.asarray(inputs["mla_cq_g"], f),
        "w_uq": np.asarray(inputs["w_uq"], f), "mla_ckv_g": np.asarray(inputs["mla_ckv_g"], f),
        "w_ukv": np.asarray(inputs["w_ukv"], f), "mla_qn_g": np.asarray(inputs["mla_qn_g"], f),
        "mla_kn_g": np.asarray(inputs["mla_kn_g"], f), "sw_g": sw_g, "sinkT": sinkT,
        "w_out": np.asarray(inputs["w_out"], f), "w_up": np.asarray(inputs["w_up"], f),
        "cwT": cwT, "cbT": cbT, "w_down": np.asarray(inputs["w_down"], f), "cbf": cbf, "cf": cf,
    }
    x = np.asarray(inputs["x"], f)
    c = np.asarray(inputs["c"], f)
    pos = np.asarray(inputs["positions"], np.int32)
    maps = []
    for core in range(n_cores):
        b0 = core * nseq
        m = dict(shared)
        m["x"] = np.ascontiguousarray(x[b0:b0 + nseq])
        m["cT"] = np.ascontiguousarray(np.transpose(c[b0:b0 + nseq].reshape(nseq, 8, 128), (2, 1, 0)))
        m["posT"] = np.ascontiguousarray(np.transpose(pos[b0:b0 + nseq].reshape(nseq, nt, 128), (0, 2, 1)))
        maps.append(m)
    return maps


_NC_CACHE = {}


def kernel(**inputs):
    x = np.asarray(inputs["x"])
    B, SEQ, _ = x.shape
    depth = np.asarray(inputs["w_in"]).shape[0]
    nseq = B // N_CORES
    key = (SEQ, depth, nseq)
    if key not in _NC_CACHE:
        _NC_CACHE[key] = build_nc(SEQ=SEQ, DEPTH=depth, NSEQ=nseq)[0]
    nc = _NC_CACHE[key]
    maps = make_in_maps(inputs, N_CORES, nseq, SEQ, depth)
    res = run_bass_kernel_spmd(nc, maps, core_ids=list(range(N_CORES)))
    out = np.concatenate([np.asarray(r["out"]) for r in res.results], axis=0)
    return out.astype(np.float32)
```
